# Optimizing a Trainium2 kernel written in Bass

```python
import math
import jax, jax.numpy as jnp
from jax import lax
import numpy as np

D_MODEL = 1024
BATCH = 2
SEQ = 16384
DEPTH = 4

GRID_W = 64
CTX_LEN = 256
N_BRANCH = 4
MIX_W = D_MODEL // 2
HEAD_DIM = 64
S5_GROUP_CH = 16
S5_GROUPS = MIX_W // S5_GROUP_CH
S5_STATE = 64
S5_DT_MIN = 1e-3
S5_DT_MAX = 1e-1
CONV_W = 3
NA_HEADS = MIX_W // HEAD_DIM
NA_ROWS = 8
NA_COLS = 16
GQA_Q_HEADS = MIX_W // HEAD_DIM
GQA_KV_HEADS = 2
GQA_KV_W = GQA_KV_HEADS * HEAD_DIM
ATTN_BLOCK = 128
WINDOW = 128
ROPE_BASE = 10000.0
ROPE_PAIRS = HEAD_DIM // 4
EPS = 1e-6
NEG_INF = -1e30
BRANCH_NAMES = ("s5", "conv", "na", "gqa")
PROJ_LAYOUT = (
    ("s5_u", MIX_W), ("s5_gate", MIX_W),
    ("conv_v", MIX_W), ("conv_b", MIX_W), ("conv_c", MIX_W), ("conv_gate", MIX_W),
    ("na_q", MIX_W), ("na_k", MIX_W), ("na_v", MIX_W), ("na_gate", MIX_W),
    ("gqa_q", MIX_W), ("gqa_k", GQA_KV_W), ("gqa_v", GQA_KV_W), ("gqa_gate", MIX_W),
    ("merge_s5", D_MODEL), ("merge_conv", D_MODEL), ("merge_na", D_MODEL), ("merge_gqa", D_MODEL),
)
N_IN = sum(size for _, size in PROJ_LAYOUT)
ALL_NAMES = tuple(name for name, _ in PROJ_LAYOUT)
CTX_KV_NAMES = ("s5_u", "na_k", "na_v", "gqa_k", "gqa_v")

kernel_name = "hybrid_s5_conv_natten_swa_dit_trunk"


def _rmsnorm(x, g):
    xf = x.astype(jnp.float32)
    y = xf * lax.rsqrt(jnp.mean(xf * xf, axis=-1, keepdims=True) + EPS)
    return (y * g.astype(jnp.float32)).astype(x.dtype)


def _adaln(cvec, w_ada, b_ada):
    mod = jax.nn.silu(cvec) @ w_ada + b_ada
    shift, scale, gate = jnp.split(mod[:, None, :], 3, axis=-1)
    return shift, scale, gate


def _project(h, w_in, names):
    out = {}
    start = 0
    for name, size in PROJ_LAYOUT:
        if name in names:
            out[name] = h @ w_in[:, start:start + size]
        start += size
    return out


def _heads(z, n_heads):
    b, n, _ = z.shape
    return z.reshape(b, n, n_heads, HEAD_DIM)


def _rope_2d(x):
    n = x.shape[1]
    t = jnp.arange(n, dtype=jnp.int32)
    row = (t // GRID_W).astype(jnp.float32)
    col = (t % GRID_W).astype(jnp.float32)
    inv = ROPE_BASE ** (-jnp.arange(ROPE_PAIRS, dtype=jnp.float32) / ROPE_PAIRS)
    ang = jnp.concatenate([row[:, None] * inv, col[:, None] * inv], axis=-1)[None, :, None, :]
    cos = jnp.cos(ang).astype(x.dtype)
    sin = jnp.sin(ang).astype(x.dtype)
    half = x.shape[-1] // 2
    x1, x2 = x[..., :half], x[..., half:]
    return jnp.concatenate([x1 * cos - x2 * sin, x2 * cos + x1 * sin], axis=-1)


def _cplx_combine(e1, e2):
    a1r, a1i, b1r, b1i = e1
    a2r, a2i, b2r, b2i = e2
    ar = a1r * a2r - a1i * a2i
    ai = a1r * a2i + a1i * a2r
    br = a2r * b1r - a2i * b1i + b2r
    bi = a2r * b1i + a2i * b1r + b2i
    return ar, ai, br, bi


def _s5_discretise(a_re, a_im, log_dt, b_re, b_im):
    f32 = jnp.float32
    a_re = a_re.astype(f32)
    a_im = a_im.astype(f32)
    dt = jnp.exp(log_dt.astype(f32))[:, None]
    mag = jnp.exp(dt * a_re)
    abr = mag * jnp.cos(dt * a_im)
    abi = mag * jnp.sin(dt * a_im)
    den = a_re * a_re + a_im * a_im
    fr = ((abr - 1.0) * a_re + abi * a_im) / den
    fi = (abi * a_re - (abr - 1.0) * a_im) / den
    b_re = b_re.astype(f32)
    b_im = b_im.astype(f32)
    bbr = fr[..., None] * b_re - fi[..., None] * b_im
    bbi = fr[..., None] * b_im + fi[..., None] * b_re
    return abr, abi, bbr, bbi


def _s5_states(ug, p, d, h0, reverse):
    abr, abi, bbr, bbi = _s5_discretise(p["s5_a_re"][d], p["s5_a_im"][d], p["s5_log_dt"][d],
                                        p["s5_b_re"][d], p["s5_b_im"][d])
    abr, abi, bbr, bbi = (t.astype(ug.dtype) for t in (abr, abi, bbr, bbi))
    xr = jnp.einsum("blgh,gph->blgp", ug, bbr)
    xi = jnp.einsum("blgh,gph->blgp", ug, bbi)
    if h0 is not None:
        h0r, h0i = h0
        first = -1 if reverse else 0
        xr = xr.at[:, first].add(abr * h0r - abi * h0i)
        xi = xi.at[:, first].add(abr * h0i + abi * h0r)
    shape = (1, ug.shape[1]) + abr.shape
    _, _, hr, hi = lax.associative_scan(
        _cplx_combine,
        (jnp.broadcast_to(abr, shape), jnp.broadcast_to(abi, shape), xr, xi),
        reverse=reverse, axis=1)
    return hr, hi


def _s5_readout(states, c_re, c_im):
    hr, hi = states
    return jnp.einsum("blgp,ghp->blgh", hr, c_re) - jnp.einsum("blgp,ghp->blgh", hi, c_im)


def _s5_output(ug, st_f, st_b, p):
    y = (_s5_readout(st_f, p["s5_c_re"][0], p["s5_c_im"][0])
         + _s5_readout(st_b, p["s5_c_re"][1], p["s5_c_im"][1])
         + p["s5_d"] * ug)
    y = jax.nn.gelu(y.reshape(ug.shape[0], ug.shape[1], MIX_W))
    return y * jax.nn.sigmoid(y @ p["s5_w_glu"])


def _groups(u):
    b, n, _ = u.shape
    return u.reshape(b, n, S5_GROUPS, S5_GROUP_CH)


def _short_conv(z, p):
    n = z["conv_v"].shape[1]
    zz = z["conv_c"] * z["conv_v"]
    pad = CONV_W // 2
    zp = jnp.pad(zz, ((0, 0), (pad, pad), (0, 0)))
    y = p["conv_b"]
    for tap in range(CONV_W):
        y = y + zp[:, tap:tap + n] * p["conv_w"][tap]
    return z["conv_b"] * y


def _ctx_attention(q, k, v, sink):
    b, n, hq, dh = q.shape
    hkv = k.shape[2]
    g = hq // hkv
    qg = q.reshape(b, n, hkv, g, dh)
    s = jnp.einsum("bqkgd,bskd->bkgqs", qg, k).astype(jnp.float32) * dh ** -0.5
    if sink is not None:
        s_sink = jnp.broadcast_to(sink.astype(jnp.float32).reshape(1, hkv, g, 1, 1), (b, hkv, g, n, 1))
        s = jnp.concatenate([s, s_sink], axis=-1)
    pr = jax.nn.softmax(s, axis=-1)[..., :k.shape[1]].astype(v.dtype)
    return jnp.einsum("bkgqs,bskd->bqkgd", pr, v).reshape(b, n, hq * dh)


def _na_latent(q, k, v, k_ctx, v_ctx, rel_bias):
    b, n, h, dh = q.shape
    rows = n // GRID_W
    kh = min(NA_ROWS, rows)
    nk = kh * GRID_W
    qg = q.reshape(b, rows, GRID_W, h, dh)
    kg = k.reshape(b, rows, GRID_W, h, dh)
    vg = v.reshape(b, rows, GRID_W, h, dh)
    qcol = jnp.arange(GRID_W, dtype=jnp.int32)
    kcol = jnp.tile(qcol, kh)
    krow = jnp.repeat(jnp.arange(kh, dtype=jnp.int32), GRID_W)
    cs = jnp.clip(qcol - NA_COLS // 2, 0, GRID_W - NA_COLS)
    col_ok = (kcol[None, :] >= cs[:, None]) & (kcol[None, :] < cs[:, None] + NA_COLS)
    dj_idx = jnp.clip(kcol[None, :] - qcol[:, None] + NA_COLS - 1, 0, 2 * NA_COLS - 2)
    scale = dh ** -0.5

    def row_block(r):
        rs = jnp.clip(r - kh // 2, 0, rows - kh)
        q_r = lax.dynamic_index_in_dim(qg, r, axis=1, keepdims=False)
        k_r = lax.dynamic_slice_in_dim(kg, rs, kh, axis=1).reshape(b, nk, h, dh)
        v_r = lax.dynamic_slice_in_dim(vg, rs, kh, axis=1).reshape(b, nk, h, dh)
        di_idx = rs + krow - r + NA_ROWS - 1
        bias = rel_bias[:, di_idx[None, :], dj_idx].astype(jnp.float32)
        s_lat = jnp.einsum("bqhd,bkhd->bhqk", q_r, k_r).astype(jnp.float32) * scale + bias
        s_lat = jnp.where(col_ok, s_lat, NEG_INF)
        s_ctx = jnp.einsum("bqhd,bkhd->bhqk", q_r, k_ctx).astype(jnp.float32) * scale
        pr = jax.nn.softmax(jnp.concatenate([s_lat, s_ctx], axis=-1), axis=-1).astype(v.dtype)
        return (jnp.einsum("bhqk,bkhd->bqhd", pr[..., :nk], v_r)
                + jnp.einsum("bhqk,bkhd->bqhd", pr[..., nk:], v_ctx))

    out = lax.map(row_block, jnp.arange(rows, dtype=jnp.int32))
    return out.transpose(1, 0, 2, 3, 4).reshape(b, n, h * dh)


def _gqa_latent(q, k, v, k_ctx, v_ctx, sink):
    b, n, hq, dh = q.shape
    hkv = k.shape[2]
    g = hq // hkv
    nb = n // ATTN_BLOCK
    span = ATTN_BLOCK + 2 * WINDOW
    pad = ((0, 0), (WINDOW, WINDOW), (0, 0), (0, 0))
    kp = jnp.pad(k, pad)
    vp = jnp.pad(v, pad)
    qi = jnp.arange(ATTN_BLOCK, dtype=jnp.int32)
    si = jnp.arange(span, dtype=jnp.int32)
    band = jnp.abs(si[None, :] - WINDOW - qi[:, None]) <= WINDOW
    sink_l = jnp.broadcast_to(sink.astype(jnp.float32).reshape(1, hkv, g, 1, 1), (b, hkv, g, ATTN_BLOCK, 1))
    n_ctx = k_ctx.shape[1]
    scale = dh ** -0.5

    def block(i):
        start = i * ATTN_BLOCK
        q_i = lax.dynamic_slice_in_dim(q, start, ATTN_BLOCK, axis=1).reshape(b, ATTN_BLOCK, hkv, g, dh)
        k_i = lax.dynamic_slice_in_dim(kp, start, span, axis=1)
        v_i = lax.dynamic_slice_in_dim(vp, start, span, axis=1)
        kpos = start - WINDOW + si
        ok = band & ((kpos >= 0) & (kpos < n))[None, :]
        s_lat = jnp.where(ok, jnp.einsum("bqkgd,bskd->bkgqs", q_i, k_i).astype(jnp.float32) * scale, NEG_INF)
        s_ctx = jnp.einsum("bqkgd,bckd->bkgqc", q_i, k_ctx).astype(jnp.float32) * scale
        pr = jax.nn.softmax(jnp.concatenate([s_lat, s_ctx, sink_l], axis=-1), axis=-1).astype(v.dtype)
        o = (jnp.einsum("bkgqs,bskd->bqkgd", pr[..., :span], v_i)
             + jnp.einsum("bkgqc,bckd->bqkgd", pr[..., span:span + n_ctx], v_ctx))
        return o.reshape(b, ATTN_BLOCK, hq * dh)

    out = lax.map(block, jnp.arange(nb, dtype=jnp.int32))
    return out.transpose(1, 0, 2, 3).reshape(b, n, hq * dh)


def _merge(z, ys, p):
    out = None
    for k, (name, y) in enumerate(zip(BRANCH_NAMES, ys)):
        gated = y * jax.nn.silu(z[name + "_gate"])
        contrib = jax.nn.sigmoid(z["merge_" + name]) * (gated @ p["w_br"][k])
        out = contrib if out is None else out + contrib
    return out @ p["w_out"]


def _layer(x, ctx, c, c_ctx, p, update_ctx):
    shift_x, scale_x, gate_x = _adaln(c, p["w_ada"], p["b_ada"])
    shift_c, scale_c, gate_c = _adaln(c_ctx[None, :], p["w_ada"], p["b_ada"])
    hx = _rmsnorm(x, p["norm_g"]) * (1.0 + scale_x) + shift_x
    hc = _rmsnorm(ctx, p["norm_g"]) * (1.0 + scale_c) + shift_c
    zx = _project(hx, p["w_in"], ALL_NAMES)
    zc = _project(hc, p["w_in"], ALL_NAMES if update_ctx else CTX_KV_NAMES)

    uc = _groups(zc["s5_u"])
    stc_f = _s5_states(uc, p, 0, None, reverse=False)
    stc_b = _s5_states(uc, p, 1, None, reverse=True)
    na_kc = _rmsnorm(_heads(zc["na_k"], NA_HEADS), p["na_k_g"])
    na_vc = _heads(zc["na_v"], NA_HEADS)
    gqa_kc = _rmsnorm(_heads(zc["gqa_k"], GQA_KV_HEADS), p["gqa_k_g"])
    gqa_vc = _heads(zc["gqa_v"], GQA_KV_HEADS)

    ux = _groups(zx["s5_u"])
    stx_f = _s5_states(ux, p, 0, (stc_f[0][:, -1], stc_f[1][:, -1]), reverse=False)
    stx_b = _s5_states(ux, p, 1, (stc_b[0][:, 0], stc_b[1][:, 0]), reverse=True)
    y_s5 = _s5_output(ux, stx_f, stx_b, p)
    y_conv = _short_conv(zx, p)
    na_q = _rmsnorm(_heads(zx["na_q"], NA_HEADS), p["na_q_g"])
    na_k = _rmsnorm(_heads(zx["na_k"], NA_HEADS), p["na_k_g"])
    na_v = _heads(zx["na_v"], NA_HEADS)
    y_na = _na_latent(na_q, na_k, na_v, na_kc, na_vc, p["na_rel_bias"])
    gqa_q = _rope_2d(_rmsnorm(_heads(zx["gqa_q"], GQA_Q_HEADS), p["gqa_q_g"]))
    gqa_k = _rope_2d(_rmsnorm(_heads(zx["gqa_k"], GQA_KV_HEADS), p["gqa_k_g"]))
    gqa_v = _heads(zx["gqa_v"], GQA_KV_HEADS)
    y_gqa = _gqa_latent(gqa_q, gqa_k, gqa_v, gqa_kc, gqa_vc, p["gqa_sink"])
    x_new = x + gate_x * _merge(zx, (y_s5, y_conv, y_na, y_gqa), p)

    if update_ctx:
        yc_s5 = _s5_output(uc, stc_f, stc_b, p)
        yc_conv = _short_conv(zc, p)
        yc_na = _ctx_attention(_rmsnorm(_heads(zc["na_q"], NA_HEADS), p["na_q_g"]), na_kc, na_vc, None)
        yc_gqa = _ctx_attention(_rmsnorm(_heads(zc["gqa_q"], GQA_Q_HEADS), p["gqa_q_g"]),
                                gqa_kc, gqa_vc, p["gqa_sink"])
        ctx = ctx + gate_c * _merge(zc, (yc_s5, yc_conv, yc_na, yc_gqa), p)
    return x_new, ctx


def setup_inputs(seed: int = 0) -> dict:
    key = jax.random.key(seed)
    ks = jax.random.split(key, 28)
    f32 = jnp.float32

    def nrm(k, shape, scale):
        return scale * jax.random.normal(k, shape, f32)

    g_p = (DEPTH, 2, S5_GROUPS, S5_STATE)
    return {
        "x": nrm(ks[0], (BATCH, SEQ, D_MODEL), 1.0),
        "c": nrm(ks[1], (BATCH, D_MODEL), 1.0),
        "ctx": nrm(ks[2], (BATCH, CTX_LEN, D_MODEL), 1.0),
        "c_ctx": nrm(ks[3], (D_MODEL,), 1.0),
        "norm_g": 1.0 + nrm(ks[4], (DEPTH, D_MODEL), 0.02),
        "w_ada": nrm(ks[5], (DEPTH, D_MODEL, 3 * D_MODEL), 0.5 * D_MODEL ** -0.5),
        "b_ada": nrm(ks[6], (DEPTH, 3 * D_MODEL), 0.02),
        "w_in": nrm(ks[7], (DEPTH, D_MODEL, N_IN), D_MODEL ** -0.5),
        "s5_a_re": -0.5 + nrm(ks[8], g_p, 0.01),
        "s5_a_im": math.pi * jnp.arange(S5_STATE, dtype=f32) + nrm(ks[9], g_p, 0.01),
        "s5_log_dt": jax.random.uniform(ks[10], (DEPTH, 2, S5_GROUPS), f32,
                                        math.log(S5_DT_MIN), math.log(S5_DT_MAX)),
        "s5_b_re": nrm(ks[11], (DEPTH, 2, S5_GROUPS, S5_STATE, S5_GROUP_CH), (2 * S5_GROUP_CH) ** -0.5),
        "s5_b_im": nrm(ks[12], (DEPTH, 2, S5_GROUPS, S5_STATE, S5_GROUP_CH), (2 * S5_GROUP_CH) ** -0.5),
        "s5_c_re": nrm(ks[13], (DEPTH, 2, S5_GROUPS, S5_GROUP_CH, S5_STATE), 0.5),
        "s5_c_im": nrm(ks[14], (DEPTH, 2, S5_GROUPS, S5_GROUP_CH, S5_STATE), 0.5),
        "s5_d": nrm(ks[15], (DEPTH, S5_GROUPS, S5_GROUP_CH), 1.0),
        "s5_w_glu": nrm(ks[16], (DEPTH, MIX_W, MIX_W), MIX_W ** -0.5),
        "conv_w": nrm(ks[17], (DEPTH, CONV_W, MIX_W), CONV_W ** -0.5),
        "conv_b": nrm(ks[18], (DEPTH, MIX_W), 0.02),
        "na_q_g": 1.0 + nrm(ks[19], (DEPTH, HEAD_DIM), 0.02),
        "na_k_g": 1.0 + nrm(ks[20], (DEPTH, HEAD_DIM), 0.02),
        "na_rel_bias": nrm(ks[21], (DEPTH, NA_HEADS, 2 * NA_ROWS - 1, 2 * NA_COLS - 1), 0.1),
        "gqa_q_g": 1.0 + nrm(ks[22], (DEPTH, HEAD_DIM), 0.02),
        "gqa_k_g": 1.0 + nrm(ks[23], (DEPTH, HEAD_DIM), 0.02),
        "gqa_sink": nrm(ks[24], (DEPTH, GQA_Q_HEADS), 0.5),
        "w_br": nrm(ks[25], (DEPTH, N_BRANCH, MIX_W, D_MODEL), MIX_W ** -0.5),
        "w_out": nrm(ks[26], (DEPTH, D_MODEL, D_MODEL), D_MODEL ** -0.5),
    }


def reference(x, c, ctx, c_ctx, norm_g, w_ada, b_ada, w_in, s5_a_re, s5_a_im, s5_log_dt,
              s5_b_re, s5_b_im, s5_c_re, s5_c_im, s5_d, s5_w_glu, conv_w, conv_b,
              na_q_g, na_k_g, na_rel_bias, gqa_q_g, gqa_k_g, gqa_sink, w_br, w_out):
    for layer in range(DEPTH):
        p = {
            "norm_g": norm_g[layer], "w_ada": w_ada[layer], "b_ada": b_ada[layer], "w_in": w_in[layer],
            "s5_a_re": s5_a_re[layer], "s5_a_im": s5_a_im[layer], "s5_log_dt": s5_log_dt[layer],
            "s5_b_re": s5_b_re[layer], "s5_b_im": s5_b_im[layer],
            "s5_c_re": s5_c_re[layer], "s5_c_im": s5_c_im[layer],
            "s5_d": s5_d[layer], "s5_w_glu": s5_w_glu[layer],
            "conv_w": conv_w[layer], "conv_b": conv_b[layer],
            "na_q_g": na_q_g[layer], "na_k_g": na_k_g[layer], "na_rel_bias": na_rel_bias[layer],
            "gqa_q_g": gqa_q_g[layer], "gqa_k_g": gqa_k_g[layer], "gqa_sink": gqa_sink[layer],
            "w_br": w_br[layer], "w_out": w_out[layer],
        }
        x, ctx = _layer(x, ctx, c, c_ctx, p, update_ctx=layer < DEPTH - 1)
    return x
```

```python
import math
from contextlib import ExitStack
import numpy as np
import concourse.bass as bass
import concourse.mybir as mybir
from concourse.bass_utils import run_bass_kernel_spmd

F32 = mybir.dt.float32
BF16 = mybir.dt.bfloat16
AF = mybir.ActivationFunctionType
ALU = mybir.AluOpType
AX = mybir.AxisListType

D = 1024
NIN = 10496
CTXN = 256
MIX = 512
TWO_PI = 2.0 * math.pi
MAGIC = 12582912.0
NEGB = -240000.0

COMPUTE = ("pe", "act", "dve", "pool")
QUEUES = ("sp", "actq", "poolq")
STREAM_OF = {"pe": "pe", "act": "act", "dve": "dve", "pool": "pool", "sp": "sp", "actq": "act", "poolq": "pool"}


import types as _types


def _freeze(fn):
    if fn.__closure__ is None:
        return fn
    cells = []
    for c in fn.__closure__:
        try:
            cells.append(_types.CellType(c.cell_contents))
        except ValueError:
            cells.append(c)
    return _types.FunctionType(fn.__code__, fn.__globals__, fn.__name__, fn.__defaults__, tuple(cells))


class Op:
    __slots__ = ("eng", "fn", "reads", "writes", "deps", "is_dma", "sig", "sigval", "dsem", "dval", "stream")

    def __init__(self, eng, fn, reads, writes):
        self.eng = eng
        self.fn = fn
        self.reads = reads
        self.writes = writes
        self.deps = []
        self.is_dma = eng in QUEUES
        self.sig = False
        self.sigval = 0
        self.dsem = None
        self.dval = 0
        self.stream = STREAM_OF[eng]


class Prog:
    def __init__(self, nc):
        self.nc = nc
        self.ops = []
        self.last_w = {}
        self.readers = {}
        self.dma_count = {}
        self.dma_last = {}
        self.last_eng = {}
        self.fence = []
        self.alias = {}

    def barrier(self):
        self.fence = list(self.last_eng.values()) + list(self.dma_last.values())

    def op(self, eng, fn, reads=(), writes=()):
        al = self.alias
        reads = tuple(dict.fromkeys(al.get(k, k) for k in reads))
        writes = tuple(dict.fromkeys(al.get(k, k) for k in writes))
        o = Op(eng, _freeze(fn), reads, writes)
        deps = list(self.fence)
        for b in o.reads:
            w = self.last_w.get(b)
            if w is not None:
                deps.append(w)
        for b in o.writes:
            w = self.last_w.get(b)
            if w is not None:
                deps.append(w)
            deps.extend(self.readers.get(b, ()))
        seen = set()
        for d in deps:
            if id(d) in seen or d is o:
                continue
            seen.add(id(d))
            if (not d.is_dma) and (not o.is_dma) and d.eng == "pe" and o.eng == "pe":
                continue
            o.deps.append(d)
        for b in o.writes:
            self.last_w[b] = o
            self.readers[b] = []
        for b in o.reads:
            self.readers.setdefault(b, []).append(o)
        if o.is_dma:
            assert len(o.writes) == 1, "dma must write exactly one key"
            k = o.writes[0]
            self.dma_count[k] = self.dma_count.get(k, 0) + 1
            o.dsem = k
            o.dval = 16 * self.dma_count[k]
            self.dma_last[k] = o
        else:
            self.last_eng[eng] = o
        self.ops.append(o)
        return o

    def pe(self, fn, reads=(), writes=()):
        return self.op("pe", fn, reads, writes)

    def act(self, fn, reads=(), writes=()):
        return self.op("act", fn, reads, writes)

    def dve(self, fn, reads=(), writes=()):
        return self.op("dve", fn, reads, writes)

    def pool(self, fn, reads=(), writes=()):
        return self.op("pool", fn, reads, writes)

    def dma(self, out, in_, reads, writes, q="sp", **kw):
        kw.setdefault("allow_slow_non_contiguous", True)
        return self.op(q, lambda e: e.dma_start(out=out, in_=in_, **kw), reads, writes)

    def emit(self, final_keys=()):
        nc = self.nc
        for o in self.ops:
            for d in o.deps:
                if not d.is_dma:
                    d.sig = True
        cnt = {e: 0 for e in COMPUTE}
        for o in self.ops:
            if (not o.is_dma) and o.sig:
                cnt[o.eng] += 1
                o.sigval = cnt[o.eng]
        dma_keys = list(self.dma_count.keys())
        with ExitStack() as st:
            import os as _os3
            _skip = [st.enter_context(nc.semaphore("skip%d" % i)) for i in range(int(_os3.environ.get("SEM_SKIP", "0")))]
            esem = {e: st.enter_context(nc.semaphore("s_" + e)) for e in COMPUTE}
            dsem = {k: st.enter_context(nc.semaphore("d%d" % i)) for i, k in enumerate(dma_keys)}
            block = st.enter_context(nc.Block())
            streams = {"pe": [], "act": [], "dve": [], "pool": [], "sp": []}
            for o in self.ops:
                streams[o.stream].append(o)
            final = [(k, 16 * self.dma_count[k]) for k in final_keys]

            def run_stream(name, e, last=False):
                waited = {}
                for o in streams[name]:
                    for d in o.deps:
                        if d.is_dma:
                            key, val, sem = ("d", d.dsem), d.dval, dsem[d.dsem]
                        else:
                            key, val, sem = ("e", d.eng), d.sigval, esem[d.eng]
                        if waited.get(key, 0) >= val:
                            continue
                        waited[key] = val
                        e.wait_ge(sem, val)
                    ins = o.fn(e)
                    if o.is_dma:
                        ins.then_inc(dsem[o.dsem], 16)
                    elif o.sig:
                        ins.then_inc(esem[o.eng], 1)
                if last:
                    for k, v in final:
                        e.wait_ge(dsem[k], v)

            @block.tensor
            def _(e):
                run_stream("pe", e)

            @block.scalar
            def _(e):
                run_stream("act", e)

            @block.vector
            def _(e):
                run_stream("dve", e)

            @block.gpsimd
            def _(e):
                run_stream("pool", e)

            @block.sync
            def _(e):
                run_stream("sp", e, True)
        return cnt, len(dma_keys)


OFF = {}
_o = 0
for _n, _s in (("s5_u", 512), ("s5_gate", 512), ("conv_v", 512), ("conv_b", 512), ("conv_c", 512), ("conv_gate", 512),
               ("na_q", 512), ("na_k", 512), ("na_v", 512), ("na_gate", 512), ("gqa_q", 512), ("gqa_k", 128),
               ("gqa_v", 128), ("gqa_gate", 512), ("merge_s5", 1024), ("merge_conv", 1024), ("merge_na", 1024),
               ("merge_gqa", 1024)):
    OFF[_n] = _o
    _o += _s
assert _o == NIN


def host_consts(SEQ):
    NT = CTXN + SEQ
    NCH = NT // 8
    NSUB = SEQ // 128
    c = {}
    c["ident"] = np.eye(128, dtype=np.float32)
    p = np.arange(128)
    c["bd16"] = (p[:, None] // 16 == p[None, :] // 16).astype(np.float32)
    c["gmask"] = (p[:, None] // 16 == np.arange(8)[None, :]).astype(np.float32)
    qc = np.arange(64)
    cs = np.clip(qc - 8, 0, 48)
    c["colok"] = np.ascontiguousarray(((qc[None, :] * 0 + np.arange(64)[None, :] >= cs[:, None]) &
                  (np.arange(64)[None, :] < cs[:, None] + 16)).astype(np.float32)[::-1])
    ROWS = SEQ // 64
    wm = np.zeros((3, 8, 128, 512), np.float32)
    for typ in range(3):
        r0 = {0: 8 if ROWS > 16 else 0, 1: 0, 2: ROWS - 8}[typ]
        for j in range(8):
            for krl in range(2):
                kr = r0 - 4 + 2 * j + krl
                for qr in range(8):
                    r = r0 + qr
                    if typ == 0:
                        ok = -4 <= kr - r <= 3
                    else:
                        rs = min(max(r - 4, 0), ROWS - 8)
                        ok = (rs <= kr < rs + 8) and (0 <= kr < ROWS)
                    if ok:
                        wm[typ, j, krl * 64:(krl + 1) * 64, qr * 64:(qr + 1) * 64] = 1.0
    c["winmask"] = wm
    ks = np.arange(128)[:, None]
    qi = np.arange(128)[None, :]
    c["tri"] = np.stack([(qi <= ks), (ks <= qi)]).astype(np.float32)
    cf = np.arange(NCH, dtype=np.float32)
    cb = np.where(np.arange(NCH) < 32, 31 - np.arange(NCH), NCH + 31 - np.arange(NCH)).astype(np.float32)
    c["qrow"] = np.stack([np.tile(cf, (128, 1)), np.tile(cb, (128, 1))])
    c["rowf"] = (2 * np.arange(NSUB)[None, :] + (p[:, None] // 64)).astype(np.float32)
    c["colv"] = (p % 64).astype(np.float32)[:, None]
    inv = (10000.0 ** (-np.arange(16, dtype=np.float64) / 16)).astype(np.float32)
    c["invf"] = np.tile(inv[None, :] / np.float32(TWO_PI), (128, 1)).astype(np.float32)
    return c


CONST_SHAPES = lambda SEQ: {k: v.shape for k, v in host_consts(SEQ).items()}


class Rot:
    def __init__(self, items):
        self.items = items
        self.i = 0

    def next(self):
        it = self.items[self.i % len(self.items)]
        self.i += 1
        return it


def build(SEQ, DEPTH, debug=(), stop=99):
    nc = bass.Bass("TRN2", target_bir_lowering=False)
    NT = CTXN + SEQ
    ROWS = SEQ // 64
    NCH = NT // 8
    NSUB = SEQ // 128
    NXT = SEQ // 512
    P = Prog(nc)
    for k_ in ("ident", "bd16", "gmask", "craw0", "craw1", "trif", "rowf", "colv", "invf", "colok", "rbt", ("ltmp", 0), ("ltmp", 1), ("ltmp", 2), ("ltmp", 3)):
        P.alias[k_] = "g_const"
    for k_ in ["badaT", "badaG", "normg", "convb", "s5d", "sinkb", "cw12"] + [("g4", i) for i in range(4)] + [("convw", i) for i in range(3)]:
        P.alias[k_] = "g_lsetup"
    for k_ in (0, 512, 1024, 1536, 2048, 3584):
        P.alias[("w1", k_)] = "w1"
    for k_ in [("s5_a_re", 0), ("s5_a_re", 1), ("s5_a_im", 0), ("s5_a_im", 1), "ldt", ("Bst", 0), ("Bst", 1), ("Btl", 0), ("Btl", 1)]:
        P.alias[k_] = "g_s5setup"
    for l_ in range(DEPTH):
        P.alias[("w_in_bf", l_)] = "w_in_bf"
        for n_ in ("w_br_bf", "w_out_bf", "w_glu_bf"):
            P.alias[(n_, l_)] = "w_small_bf"

    def din(name, shape):
        return nc.dram_tensor(name, list(shape), F32, kind="ExternalInput").ap()

    def dscr(name, shape, dt):
        kind = "ExternalOutput" if name in debug else "Internal"
        return nc.dram_tensor(name, list(shape), dt, kind=kind).ap()

    I = {}
    I["x"] = din("x", [SEQ, D]); I["c"] = din("c", [D]); I["ctx"] = din("ctx", [CTXN, D]); I["c_ctx"] = din("c_ctx", [D])
    I["norm_g"] = din("norm_g", [DEPTH, D]); I["w_ada"] = din("w_ada", [DEPTH, D, 3 * D]); I["b_ada"] = din("b_ada", [DEPTH, 3 * D])
    I["w_in"] = din("w_in", [DEPTH, D, NIN])
    for n in ("s5_a_re", "s5_a_im"):
        I[n] = din(n, [DEPTH, 2, 32, 64])
    I["s5_log_dt"] = din("s5_log_dt", [DEPTH, 2, 32])
    for n in ("s5_b_re", "s5_b_im"):
        I[n] = din(n, [DEPTH, 2, 32, 64, 16])
    for n in ("s5_c_re", "s5_c_im"):
        I[n] = din(n, [DEPTH, 2, 32, 16, 64])
    I["s5_d"] = din("s5_d", [DEPTH, 32, 16]); I["s5_w_glu"] = din("s5_w_glu", [DEPTH, MIX, MIX])
    I["conv_w"] = din("conv_w", [DEPTH, 3, MIX]); I["conv_b"] = din("conv_b", [DEPTH, MIX])
    for n in ("na_q_g", "na_k_g", "gqa_q_g", "gqa_k_g"):
        I[n] = din(n, [DEPTH, 64])
    I["na_rel_bias"] = din("na_rel_bias", [DEPTH, 8, 15, 31]); I["gqa_sink"] = din("gqa_sink", [DEPTH, 8])
    I["w_br"] = din("w_br", [DEPTH, 4, MIX, D]); I["w_out"] = din("w_out", [DEPTH, D, D])
    C = {k: din("k_" + k, shp) for k, shp in CONST_SHAPES(SEQ).items()}
    y_out = nc.dram_tensor("y", [SEQ, D], F32, kind="ExternalOutput").ap()

    w_in_bf = dscr("w_in_bf", [DEPTH, D, NIN], BF16)
    w_br_bf = dscr("w_br_bf", [DEPTH, 4, MIX, D], BF16)
    w_out_bf = dscr("w_out_bf", [DEPTH, D, D], BF16)
    w_glu_bf = dscr("w_glu_bf", [DEPTH, MIX, MIX], BF16)
    xres = dscr("xres", [SEQ, D], F32)
    cres = dscr("cres", [CTXN, D], F32)
    hTd = dscr("hTd", [D, NT], BF16)
    uTd = dscr("uTd", [MIX, NT], BF16)
    zzTd = dscr("zzTd", [MIX, NT], BF16)
    cbTd = dscr("cbTd", [MIX, NT], BF16)
    naqTd = dscr("naqTd", [MIX, NT], BF16)
    nakTd = dscr("nakTd", [MIX, NT], BF16)
    gqd = dscr("gqd", [64, 8, NT], BF16)
    gkd = dscr("gkd", [64, 2, NT], BF16)
    navd = dscr("navd", [NT, MIX], BF16)
    gvd = dscr("gvd", [NT, 192], BF16)
    ys5Td = dscr("ys5Td", [MIX, NT], BF16)
    Hind = dscr("Hind", [64, 128, NCH], BF16)
    ropeCd = dscr("ropeCd", [NSUB, 128, 32], F32)
    ropeSd = dscr("ropeSd", [NSUB, 128, 32], F32)

    with ExitStack() as glob:
        _uid = [0]

        def sb(name, shape, dt, st=None):
            _uid[0] += 1
            return (st or glob).enter_context(nc.sbuf_tensor("%s_%d" % (name, _uid[0]), list(shape), dt))

        psf = [glob.enter_context(nc.psum_tensor("psf%d" % i, [128, 512], F32)) for i in range(6)]
        psb = [glob.enter_context(nc.psum_tensor("psb%d" % i, [128, 1024], BF16)) for i in range(2)]
        work = Rot([(psf[i], "psf%d" % i) for i in range(3)])
        accC = (psf[3], "psf3")
        accA = (psf[4], "psf4")
        accB = (psf[5], "psf5")
        tb = Rot([(psb[i], "psb%d" % i) for i in range(2)])

        ident = sb("ident", [128, 128], F32)
        identb = sb("identb", [128, 128], BF16)
        onesb = sb("onesb", [128, 128], BF16)
        bd16 = sb("bd16", [128, 128], F32)
        gmask = sb("gmask", [128, 8], F32)
        sgn2 = sb("sgn2", [128, 1], F32)
        sgnB = sb("sgnB", [128, 1], F32)
        sc2 = sb("sc2", [128, 8, 2], F32)
        trib = sb("trib", [128, 2, 128], BF16)
        P.dma(ident[:], C["ident"][:, :], [], ["ident"])
        P.dma(bd16[:], C["bd16"][:, :], [], ["bd16"])
        P.dma(gmask[:], C["gmask"][:, :], [], ["gmask"])
        P.dve(lambda e: e.tensor_copy(out=identb[:], in_=ident[:]), ["ident"], ["identb"])
        P.dve(lambda e: e.memset(onesb[:], 1.0), [], ["onesb"])
        P.dve(lambda e: e.memset(sgn2[0:64, :], 1.0), [], ["sgn2"])
        P.dve(lambda e: e.memset(sgn2[64:128, :], -1.0), [], ["sgn2"])
        P.dve(lambda e: e.memset(sgnB[0:64, :], -1.0), [], ["sgnB"])
        P.dve(lambda e: e.memset(sgnB[64:128, :], 1.0), [], ["sgnB"])

        for l in range(DEPTH):
            for k in range(8):
                P.dma(w_in_bf[l, 128 * k:128 * (k + 1), :], I["w_in"][l, 128 * k:128 * (k + 1), :], [], [("w_in_bf", l)], q="poolq")
            for b in range(4):
                P.dma(w_br_bf[l, b], I["w_br"][l, b], [], [("w_br_bf", l)], q="poolq")
            for k in range(2):
                P.dma(w_out_bf[l, 512 * k:512 * (k + 1), :], I["w_out"][l, 512 * k:512 * (k + 1), :], [], [("w_out_bf", l)], q="poolq")
            P.dma(w_glu_bf[l], I["s5_w_glu"][l], [], [("w_glu_bf", l)], q="poolq")

        def sin_turns(out, t, tmp, keys_t, key_out, key_tmp, scale=TWO_PI):
            P.dve(lambda e: e.tensor_scalar(out=tmp, in0=t, scalar1=MAGIC, scalar2=MAGIC, op0=ALU.add, op1=ALU.subtract), keys_t, [key_tmp])
            P.dve(lambda e: e.tensor_tensor(out=tmp, in0=t, in1=tmp, op=ALU.subtract), list(keys_t) + [key_tmp], [key_tmp])
            P.act(lambda e: e.activation(out=out, in_=tmp, func=AF.Sin, scale=scale), [key_tmp], [key_out])

        ltmp_key = [0]

        def load_T(dst, src2d, n, dkey, st):
            ltmp_key[0] += 1
            tmpT = sb("ltmp", [n, 128], F32, st)
            tk_ = ("ltmp", ltmp_key[0] % 4)
            P.dma(tmpT[:], src2d, [], [tk_])
            ps, pk = work.next()
            P.pe(lambda e: e.transpose(out=ps[:, 0:n], in_=tmpT[:], identity=ident[0:n, 0:n]), [tk_, "ident"], [pk])
            P.dve(lambda e: e.tensor_copy(out=dst, in_=ps[:, 0:n]), [pk], [dkey])

        with ExitStack() as s0:
            craw = sb("craw", [128, 8, 2], F32, s0)
            craw2 = sb("craw2", [128, 2, 8], F32, s0)
            load_T(craw2[:, 0, :], I["c"].rearrange("(k p) -> k p", p=128), 8, "craw0", s0)
            load_T(craw2[:, 1, :], I["c_ctx"].rearrange("(k p) -> k p", p=128), 8, "craw1", s0)
            P.dve(lambda e: e.tensor_copy(out=craw[:], in_=craw2[:].rearrange("p m k -> p k m")), ["craw0", "craw1"], ["craw0"])
            P.act(lambda e: e.activation(out=sc2[:], in_=craw[:], func=AF.Silu), ["craw0", "craw1"], ["sc2"])
            trif = sb("trif", [128, 2, 128], F32, s0)
            P.dma(trif[:], C["tri"].rearrange("a k q -> k a q"), [], ["trif"])
            P.dve(lambda e: e.tensor_copy(out=trib[:], in_=trif[:]), ["trif"], ["trib"])
            rowf = sb("rowf", [128, NSUB], F32, s0)
            colv = sb("colv", [128, 1], F32, s0)
            invf = sb("invf", [128, 16], F32, s0)
            ang = sb("ang", [128, NSUB, 32], F32, s0)
            ang2 = sb("ang2", [128, NSUB, 32], F32, s0)
            tmpr = sb("tmpr", [128, NSUB, 32], F32, s0)
            rc = sb("rc", [128, NSUB, 32], F32, s0)
            rs = sb("rs", [128, NSUB, 32], F32, s0)
            P.dma(rowf[:], C["rowf"][:, :], [], ["rowf"])
            P.dma(colv[:], C["colv"][:, :], [], ["colv"])
            P.dma(invf[:], C["invf"][:, :], [], ["invf"])
            P.dve(lambda e: e.tensor_tensor(out=ang[:, :, 0:16], in0=invf[:].unsqueeze(1).to_broadcast([128, NSUB, 16]),
                                            in1=rowf[:].unsqueeze(2).to_broadcast([128, NSUB, 16]), op=ALU.mult), ["invf", "rowf"], ["ang"])
            P.dve(lambda e: e.tensor_scalar(out=ang[:, :, 16:32], in0=invf[:].unsqueeze(1).to_broadcast([128, NSUB, 16]),
                                            scalar1=colv[:, 0:1], scalar2=None, op0=ALU.mult), ["invf", "colv"], ["ang"])
            P.dve(lambda e: e.tensor_scalar(out=ang2[:], in0=ang[:], scalar1=0.25, scalar2=None, op0=ALU.add), ["ang"], ["ang2"])
            sin_turns(rs[:], ang[:], tmpr[:], ["ang"], "rs", "tmpr")
            sin_turns(rc[:], ang2[:], tmpr[:], ["ang2"], "rc", "tmpr")
            P.dma(ropeCd.rearrange("j p k -> p j k"), rc[:], ["rc"], ["ropeCd"])
            P.dma(ropeSd.rearrange("j p k -> p j k"), rs[:], ["rs"], ["ropeSd"])
        P.barrier()

        gm = sb("gm", [128, 8, 2], F32)
        shf = sb("shf", [128, 8, 2], F32)
        gateR = sb("gateR", [128, 2, D], F32)
        gains = sb("gains", [128, 26, 64], F32)
        convw = sb("convw", [128, 4, 3], F32)
        convb = sb("convb", [128, 4], F32)
        s5d = sb("s5d", [128, 4], F32)
        esink = sb("esink", [128, 8], F32)

        def finish_early():
            import os as _os2
            P.barrier()
            P.dma(y_out[0:128, :], gateR[:, 0, :], ["gateR"], ["y"])
            extra = [k for k in _os2.environ.get("FINALKEYS", "").split(",") if k and k in P.dma_count]
            return nc, P.emit(final_keys=["y"] + extra)

        tiles = [("ctx", 0, CTXN)] + [("x", CTXN + 512 * i, 512) for i in range(NXT)]

        def resid_src(l, kind):
            if kind == "ctx":
                return (I["ctx"], "in_ctx") if l == 0 else (cres, "cres")
            return (I["x"], "in_x") if l == 0 else (xres, "xres")

        def resid_dst(l, kind):
            if kind == "ctx":
                return (cres, "cres")
            return (y_out, "y") if l == DEPTH - 1 else (xres, "xres")

        for l in range(DEPTH):
            with ExitStack() as sA:
                wada = sb("wada", [128, 8, D], F32, sA)
                screp = sb("screp", [128, 2, 8, 128], F32, sA)
                for m in range(2):
                    P.dve(lambda e, m=m: e.tensor_copy(out=screp[:, m], in_=sc2[:, :, m:m + 1].to_broadcast([128, 8, 128])), ["sc2"], ["screp"])
                badaT = sb("badaT", [128, 24], F32, sA)
                badaG = sb("badaG", [128, D], F32, sA)
                normg = sb("normg", [128, 8], F32, sA)
                modT = sb("modT", [128, 2, 8, 2], F32, sA)
                sinkb = sb("sinkb", [128, 8], F32, sA)
                g4 = sb("g4", [128, 4, 64], F32, sA)
                load_T(badaT[:], I["b_ada"][l].rearrange("(a p) -> a p", p=128), 24, "badaT", sA)
                P.dma(badaG[:], I["b_ada"][l, 2 * D:3 * D].partition_broadcast(128), [], ["badaG"])
                load_T(normg[:], I["norm_g"][l].rearrange("(k p) -> k p", p=128), 8, "normg", sA)
                for part in range(3):
                    P.dma(wada[:], I["w_ada"][l, :, part * D:(part + 1) * D].rearrange("(k p) n -> p k n", p=128), [], ["wada"])
                    if part < 2:
                        for cc in range(8):
                            ps, pk = work.next()
                            for k in range(8):
                                P.pe(lambda e, ps=ps, k=k, cc=cc: e.matmul(ps[:, 0:2], lhsT=wada[:, k, 128 * cc:128 * cc + 128], rhs=sc2[:, k, :],
                                                                          start=(k == 0), stop=(k == 7)), ["wada", "sc2"], [pk])
                            P.dve(lambda e, ps=ps, part=part, cc=cc: e.tensor_scalar(out=modT[:, part, cc, :], in0=ps[:, 0:2],
                                                                                    scalar1=badaT[:, part * 8 + cc:part * 8 + cc + 1], scalar2=None, op0=ALU.add),
                                  [pk, "badaT"], ["modT"])
                    else:
                        for m in range(2):
                            for half in range(2):
                                ps, pk = work.next()
                                for k in range(8):
                                    P.pe(lambda e, ps=ps, k=k, m=m, half=half: e.matmul(ps[:, :], lhsT=screp[:, m, k, :], rhs=wada[:, k, 512 * half:512 * half + 512],
                                                                                       start=(k == 0), stop=(k == 7)), ["wada", "screp"], [pk])
                                P.dve(lambda e, ps=ps, m=m, half=half: e.tensor_tensor(out=gateR[:, m, 512 * half:512 * half + 512], in0=ps[:, :],
                                                                                      in1=badaG[:, 512 * half:512 * half + 512], op=ALU.add), [pk, "badaG"], ["gateR"])
                P.dve(lambda e: e.tensor_copy(out=shf[:], in_=modT[:, 0]), ["modT"], ["shf"])
                P.dve(lambda e: e.scalar_tensor_tensor(out=gm[:], in0=modT[:, 1], scalar=1.0, in1=normg[:].unsqueeze(2).to_broadcast([128, 8, 2]),
                                                       op0=ALU.add, op1=ALU.mult), ["modT", "normg"], ["gm"])
                for i, n in enumerate(("na_q_g", "na_k_g", "gqa_q_g", "gqa_k_g")):
                    P.dma(g4[:, i, :], I[n][l].partition_broadcast(128), [], [("g4", i)])
                for i, (h0, nh) in enumerate(((0, 8), (8, 8), (16, 8), (24, 2))):
                    P.dve(lambda e, i=i, h0=h0, nh=nh: e.tensor_copy(out=gains[:, h0:h0 + nh, :], in_=g4[:, i:i + 1, :].to_broadcast([128, nh, 64])),
                          [("g4", i)], ["gains"])
                cw12 = sb("cw12", [128, 12], F32, sA)
                load_T(cw12[:], I["conv_w"][l].rearrange("t (c p) -> (t c) p", p=128), 12, "cw12", sA)
                P.dve(lambda e: e.tensor_copy(out=convw[:], in_=cw12[:].rearrange("p (t c) -> p c t", t=3)), ["cw12"], [("convw", 0)])
                load_T(convb[:], I["conv_b"][l].rearrange("(c p) -> c p", p=128), 4, "convb", sA)
                load_T(s5d[:], I["s5_d"][l].rearrange("g h -> (g h)").rearrange("(c p) -> c p", p=128), 4, "s5d", sA)
                P.dma(sinkb[:], I["gqa_sink"][l].partition_broadcast(128), [], ["sinkb"])
                P.act(lambda e: e.activation(out=esink[:], in_=sinkb[:], func=AF.Exp), ["sinkb"], ["esink"])
            P.barrier()
            if stop == 1:
                return finish_early()

            with ExitStack() as s1:
                w1 = sb("w1", [128, 8, 4352], BF16, s1)
                xt = sb("xt", [128, 4, D], F32, s1)
                sq = sb("sq", [128, 4, D], F32, s1)
                ssq = sb("ssq", [128, 4], F32, s1)
                xn = sb("xn", [128, 4, D], BF16, s1)
                hT = sb("hT", [128, 8, 512], BF16, s1)
                uT_st = sb("uT_st", [128, 4, 512], BF16, s1)
                zz_st = sb("zz_st", [128, 4, 512], BF16, s1)
                cb_st = sb("cb_st", [128, 4, 512], BF16, s1)
                vtmp = sb("vtmp", [128, 512], F32, s1)
                tm = sb("tm", [128, 1664], F32, s1)
                tsq = sb("tsq", [128, 1664], F32, s1)
                hss = sb("hss", [128, 26], F32, s1)
                qkb = sb("qkb", [128, 1664], BF16, s1)
                rt = sb("rt", [128, 4, 320], F32, s1)
                ropeC = sb("ropeC", [128, 4, 32], F32, s1)
                ropeS = sb("ropeS", [128, 4, 32], F32, s1)
                naq_st = sb("naq_st", [128, 4, 512], BF16, s1)
                nak_st = sb("nak_st", [128, 4, 512], BF16, s1)
                gq_st = sb("gq_st", [64, 8, 512], BF16, s1)
                gk_st = sb("gk_st", [64, 2, 512], BF16, s1)
                v_st = sb("v_st", [128, 4, 512], BF16, s1)
                gv_st = sb("gv_st", [128, 4, 192], BF16, s1)
                wsrc = w_in_bf[l].rearrange("(k p) n -> p k n", p=128)
                for (d0, s0_, n_) in ((0, OFF["s5_u"], 512), (512, OFF["conv_v"], 512), (1024, OFF["conv_b"], 512), (1536, OFF["conv_c"], 512),
                                      (2048, OFF["na_q"], 1536), (3584, OFF["gqa_q"], 768)):
                    P.dma(w1[:, :, d0:d0 + n_], wsrc[:, :, s0_:s0_ + n_], [("w_in_bf", l)], [("w1", d0)])
                w1keys = [("w1", d0) for d0 in (0, 512, 1024, 1536, 2048, 3584)]

                import os as _os
                p1tiles = (tiles[1:] + tiles[:1]) if _os.environ.get("XFIRST") else tiles
                for _tidx, (kind, n0, ntok) in enumerate(p1tiles):
                    nsub = ntok // 128
                    mi = 1 if kind == "ctx" else 0
                    src, skey = resid_src(l, kind)
                    r0 = n0 if kind == "ctx" else n0 - CTXN
                    P.dma(xt[:, 0:nsub, :], src[r0:r0 + ntok, :].rearrange("(s p) d -> p s d", p=128), [skey], ["xt"])
                    if kind == "x":
                        j0 = r0 // 128
                        P.dma(ropeC[:, 0:nsub, :], ropeCd[j0:j0 + nsub].rearrange("j p k -> p j k"), ["ropeCd"], ["ropeC"])
                        P.dma(ropeS[:, 0:nsub, :], ropeSd[j0:j0 + nsub].rearrange("j p k -> p j k"), ["ropeSd"], ["ropeS"])
                    P.act(lambda e, nsub=nsub: e.activation(out=sq[:, 0:nsub, :], in_=xt[:, 0:nsub, :], func=AF.Square), ["xt"], ["sq"])
                    P.dve(lambda e, nsub=nsub: e.tensor_reduce(out=ssq[:, 0:nsub], in_=sq[:, 0:nsub, :], axis=AX.X, op=ALU.add), ["sq"], ["ssq"])
                    P.act(lambda e, nsub=nsub: e.activation(out=ssq[:, 0:nsub], in_=ssq[:, 0:nsub], func=AF.Sqrt, scale=1.0 / D, bias=1e-6), ["ssq"], ["ssq"])
                    P.dve(lambda e, nsub=nsub: e.reciprocal(out=ssq[:, 0:nsub], in_=ssq[:, 0:nsub]), ["ssq"], ["ssq"])
                    for s in range(nsub):
                        P.dve(lambda e, s=s: e.tensor_scalar(out=xn[:, s, :], in0=xt[:, s, :], scalar1=ssq[:, s:s + 1], scalar2=None, op0=ALU.mult),
                              ["xt", "ssq"], [("xn", s)])
                    if stop == 21 or (stop == 41 and kind == "x"):
                        return finish_early()
                    for k in range(8):
                        pt, ptk = tb.next()
                        for s in range(nsub):
                            P.pe(lambda e, pt=pt, k=k, s=s: e.transpose(out=pt[:, 128 * s:128 * s + 128], in_=xn[:, s, 128 * k:128 * k + 128], identity=identb[:]),
                                 [("xn", s), "identb"], [ptk])
                        P.act(lambda e, pt=pt, k=k, ntok=ntok, mi=mi: e.activation(out=hT[:, k, 0:ntok], in_=pt[:, 0:ntok], func=AF.Identity,
                                                                                  scale=gm[:, k, mi:mi + 1], bias=shf[:, k, mi:mi + 1]),
                              [ptk, "gm", "shf"], ["hT"])
                    P.dma(hTd.rearrange("(k p) n -> p k n", p=128)[:, :, n0:n0 + ntok], hT[:, :, 0:ntok], ["hT"], ["hTd"], q="sp")
                    if stop == 22 or (stop == 42 and kind == "x"):
                        return finish_early()
                    for gi, (wc0, name) in enumerate(((0, "u"), (512, "v"), (1536, "c"), (1024, "b"))):
                        for cc in range(4):
                            ps, pk = work.next()
                            for k in range(8):
                                P.pe(lambda e, ps=ps, k=k, wc0=wc0, cc=cc, ntok=ntok: e.matmul(ps[:, 0:ntok], lhsT=w1[:, k, wc0 + 128 * cc:wc0 + 128 * cc + 128],
                                                                                            rhs=hT[:, k, 0:ntok], start=(k == 0), stop=(k == 7)),
                                     w1keys + ["hT"], [pk])
                            if name == "u":
                                P.act(lambda e, ps=ps, cc=cc, ntok=ntok: e.activation(out=uT_st[:, cc, 0:ntok], in_=ps[:, 0:ntok], func=AF.Copy), [pk], ["uT_st"])
                            elif name == "v":
                                P.act(lambda e, ps=ps, cc=cc, ntok=ntok: e.activation(out=zz_st[:, cc, 0:ntok], in_=ps[:, 0:ntok], func=AF.Copy), [pk], ["zz_st"])
                            elif name == "c":
                                P.dve(lambda e, ps=ps, cc=cc, ntok=ntok: e.tensor_copy(out=vtmp[:, 0:ntok], in_=zz_st[:, cc, 0:ntok]), ["zz_st"], ["vtmp"])
                                P.dve(lambda e, ps=ps, cc=cc, ntok=ntok: e.tensor_tensor(out=zz_st[:, cc, 0:ntok], in0=ps[:, 0:ntok], in1=vtmp[:, 0:ntok], op=ALU.mult),
                                      [pk, "vtmp"], ["zz_st"])
                            else:
                                P.act(lambda e, ps=ps, cc=cc, ntok=ntok: e.activation(out=cb_st[:, cc, 0:ntok], in_=ps[:, 0:ntok], func=AF.Copy), [pk], ["cb_st"])
                    for (dst, stg, key) in ((uTd, uT_st, "uT_st"), (zzTd, zz_st, "zz_st"), (cbTd, cb_st, "cb_st")):
                        P.dma(dst.rearrange("(c p) n -> p c n", p=128)[:, :, n0:n0 + ntok], stg[:, :, 0:ntok], [key], [key + "_d"], q="sp")
                    if stop == 23 or (stop == 43 and kind == "x"):
                        return finish_early()
                    for s in range(nsub):
                        for (c0, c1) in ((0, 512), (512, 1024), (1024, 1536), (1536, 2048), (2048, 2304)):
                            if _os.environ.get("BARRIER_TM"):
                                P.barrier()
                            ps, pk = work.next()
                            for k in range(8):
                                P.pe(lambda e, ps=ps, k=k, s=s, c0=c0, c1=c1: e.matmul(ps[:, 0:c1 - c0], lhsT=hT[:, k, 128 * s:128 * s + 128],
                                                                                      rhs=w1[:, k, 2048 + c0:2048 + c1], start=(k == 0), stop=(k == 7)),
                                     w1keys + ["hT"], [pk])
                            if c0 == 0:
                                P.act(lambda e, ps=ps: e.activation(out=tm[:, 0:512], in_=ps[:, :], func=AF.Copy), [pk], ["tm"])
                            elif c0 == 512:
                                P.act(lambda e, ps=ps: e.activation(out=tm[:, 512:1024], in_=ps[:, :], func=AF.Copy), [pk], ["tm"])
                            elif c0 == 1024:
                                P.dve(lambda e, ps=ps, s=s: e.tensor_copy(out=v_st[:, s, :], in_=ps[:, :]), [pk], ["v_st"])
                            elif c0 == 1536:
                                P.act(lambda e, ps=ps: e.activation(out=tm[:, 1024:1536], in_=ps[:, :], func=AF.Copy), [pk], ["tm"])
                            else:
                                P.dve(lambda e, ps=ps: e.tensor_copy(out=tm[:, 1536:1664], in_=ps[:, 0:128]), [pk], ["tm"])
                                P.dve(lambda e, ps=ps, s=s: e.tensor_copy(out=gv_st[:, s, 0:128], in_=ps[:, 128:256]), [pk], ["gv_st"])
                                if kind == "x" and s == 0 and stop == 56:
                                    return finish_early()
                                P.dve(lambda e, ps=ps, s=s: e.tensor_copy(out=gv_st[:, s, 128:192], in_=ps[:, 128:192]), [pk], ["gv_st"])
                            if kind == "x" and s == 0 and stop == 51 + (c0 // 512):
                                return finish_early()
                        if stop == 24 or (stop == 32 and s == 1) or (stop == 44 and kind == "x") or (stop == 60 and _tidx == 1 and s == 0):
                            return finish_early()
                        tm3 = tm[:].rearrange("p (h d) -> p h d", d=64)
                        P.pool(lambda e: e.tensor_tensor(out=tsq[:], in0=tm[:], in1=tm[:], op=ALU.mult), ["tm"], ["tsq"])
                        P.dve(lambda e: e.tensor_reduce(out=hss[:], in_=tsq[:].rearrange("p (h d) -> p h d", d=64), axis=AX.X, op=ALU.add), ["tsq"], ["hss"])
                        P.act(lambda e: e.activation(out=hss[:], in_=hss[:], func=AF.Sqrt, scale=1.0 / 64, bias=1e-6), ["hss"], ["hss"])
                        P.dve(lambda e: e.reciprocal(out=hss[:], in_=hss[:]), ["hss"], ["hss"])
                        P.dve(lambda e, tm3=tm3: e.tensor_tensor(out=tm3, in0=tm3, in1=hss[:].unsqueeze(2).to_broadcast([128, 26, 64]), op=ALU.mult),
                              ["tm", "hss"], ["tm"])
                        if stop == 25 or (stop == 45 and kind == "x"):
                            return finish_early()
                        if kind == "x":
                            P.dve(lambda e, tm3=tm3: e.tensor_tensor(out=qkb[:, 0:1024].rearrange("p (h d) -> p h d", d=64), in0=tm3[:, 0:16, :],
                                                                    in1=gains[:, 0:16, :], op=ALU.mult), ["tm", "gains"], ["qkb"])
                            P.dve(lambda e, tm3=tm3: e.tensor_tensor(out=tm3[:, 16:26, :], in0=tm3[:, 16:26, :], in1=gains[:, 16:26, :], op=ALU.mult),
                                   ["tm", "gains"], ["tm"])
                            g4v = tm[:, 1024:1664].rearrange("p (h t k) -> p h t k", t=2, k=32)
                            o4v = qkb[:, 1024:1664].rearrange("p (h t k) -> p h t k", t=2, k=32)
                            cb_ = ropeC[:, s, :].unsqueeze(1).to_broadcast([128, 10, 32])
                            sb_ = ropeS[:, s, :].unsqueeze(1).to_broadcast([128, 10, 32])
                            r4 = rt[:].rearrange("p a (h k) -> p a h k", k=32)
                            P.dve(lambda e, g4v=g4v, cb_=cb_, r4=r4: e.tensor_tensor(out=r4[:, 0], in0=g4v[:, :, 0, :], in1=cb_, op=ALU.mult), ["tm", "ropeC"], ["rt0"])
                            P.dve(lambda e, g4v=g4v, sb_=sb_, r4=r4: e.tensor_tensor(out=r4[:, 1], in0=g4v[:, :, 1, :], in1=sb_, op=ALU.mult), ["tm", "ropeS"], ["rt1"])
                            P.dve(lambda e, g4v=g4v, cb_=cb_, r4=r4: e.tensor_tensor(out=r4[:, 2], in0=g4v[:, :, 1, :], in1=cb_, op=ALU.mult), ["tm", "ropeC"], ["rt2"])
                            P.dve(lambda e, g4v=g4v, sb_=sb_, r4=r4: e.tensor_tensor(out=r4[:, 3], in0=g4v[:, :, 0, :], in1=sb_, op=ALU.mult), ["tm", "ropeS"], ["rt3"])
                            P.dve(lambda e, o4v=o4v, r4=r4: e.tensor_tensor(out=o4v[:, :, 0, :], in0=r4[:, 0], in1=r4[:, 1], op=ALU.subtract), ["rt0", "rt1"], ["qkb"])
                            P.dve(lambda e, o4v=o4v, r4=r4: e.tensor_tensor(out=o4v[:, :, 1, :], in0=r4[:, 2], in1=r4[:, 3], op=ALU.add), ["rt2", "rt3"], ["qkb"])
                        else:
                            P.dve(lambda e, tm3=tm3: e.tensor_tensor(out=qkb[:].rearrange("p (h d) -> p h d", d=64), in0=tm3, in1=gains[:], op=ALU.mult),
                                  ["tm", "gains"], ["qkb"])
                        if stop == 26 or (stop == 29 and kind == "x") or (stop == 31 and s == 1):
                            return finish_early()
                        pt, ptk = tb.next()
                        for j in range(4):
                            P.pe(lambda e, pt=pt, j=j: e.transpose(out=pt[:, 128 * j:128 * j + 128], in_=qkb[:, 128 * j:128 * j + 128], identity=identb[:]),
                                 ["qkb", "identb"], [ptk])
                        P.act(lambda e, pt=pt, s=s: e.activation(out=naq_st[:, :, 128 * s:128 * s + 128], in_=pt[:, 0:512].rearrange("p (c t) -> p c t", t=128), func=AF.Copy),
                              [ptk], ["naq_st"])
                        pt, ptk = tb.next()
                        for j in range(4):
                            P.pe(lambda e, pt=pt, j=j: e.transpose(out=pt[:, 128 * j:128 * j + 128], in_=qkb[:, 512 + 128 * j:512 + 128 * j + 128], identity=identb[:]),
                                 ["qkb", "identb"], [ptk])
                        P.dve(lambda e, pt=pt, s=s: e.tensor_copy(out=nak_st[:, :, 128 * s:128 * s + 128], in_=pt[:, 0:512].rearrange("p (c t) -> p c t", t=128)),
                              [ptk], ["nak_st"])
                        pt, ptk = tb.next()
                        for j in range(4):
                            P.pe(lambda e, pt=pt, j=j: e.transpose(out=pt[:, 128 * j:128 * j + 128], in_=qkb[:, 1024 + 128 * j:1024 + 128 * j + 128], identity=identb[:]),
                                 ["qkb", "identb"], [ptk])
                        p3 = pt[:, 0:512].rearrange("p (c t) -> p c t", t=128)
                        g4s = gq_st[:, :, 128 * s:128 * s + 128].rearrange("p (c two) t -> p c two t", two=2)
                        P.dve(lambda e, p3=p3, g4s=g4s: e.tensor_copy(out=g4s[:, :, 0, :], in_=p3[0:64]), [ptk], ["gq_st"])
                        P.dve(lambda e, p3=p3, g4s=g4s: e.tensor_copy(out=g4s[:, :, 1, :], in_=p3[64:128]), [ptk], ["gq_st"])
                        pt, ptk = tb.next()
                        P.pe(lambda e, pt=pt: e.transpose(out=pt[:, 0:128], in_=qkb[:, 1536:1664], identity=identb[:]), ["qkb", "identb"], [ptk])
                        P.dve(lambda e, pt=pt, s=s: e.tensor_copy(out=gk_st[:, 0, 128 * s:128 * s + 128], in_=pt[0:64, 0:128]), [ptk], ["gk_st"])
                        P.dve(lambda e, pt=pt, s=s: e.tensor_copy(out=gk_st[:, 1, 128 * s:128 * s + 128], in_=pt[64:128, 0:128]), [ptk], ["gk_st"])
                        if stop == 33:
                            return finish_early()
                    if stop == 27:
                        return finish_early()
                    P.dma(naqTd.rearrange("(c p) n -> p c n", p=128)[:, :, n0:n0 + ntok], naq_st[:, :, 0:ntok], ["naq_st"], ["naqTd"], q="sp")
                    P.dma(nakTd.rearrange("(c p) n -> p c n", p=128)[:, :, n0:n0 + ntok], nak_st[:, :, 0:ntok], ["nak_st"], ["nakTd"], q="sp")
                    P.dma(gqd[:, :, n0:n0 + ntok], gq_st[:, :, 0:ntok], ["gq_st"], ["gqd"], q="sp")
                    P.dma(gkd[:, :, n0:n0 + ntok], gk_st[:, :, 0:ntok], ["gk_st"], ["gkd"], q="sp")
                    P.dma(navd[n0:n0 + ntok, :].rearrange("(s p) c -> p s c", p=128), v_st[:, 0:nsub, :], ["v_st"], ["navd"], q="sp")
                    P.dma(gvd[n0:n0 + ntok, :].rearrange("(s p) c -> p s c", p=128), gv_st[:, 0:nsub, :], ["gv_st"], ["gvd"], q="sp")
                    if stop == 28:
                        return finish_early()
            P.barrier()
            if stop == 2:
                return finish_early()

            with ExitStack() as s5:
                DG = 64
                PA = sb("PA", [128, 9, DG], F32, s5)
                PB = sb("PB", [128, 9, DG], F32, s5)
                tfr = sb("tfr", [128, DG], F32, s5)
                rho8 = sb("rho8", [128, DG], F32, s5)
                Bbar = sb("Bbar", [128, DG, 16], F32, s5)
                Bbart = sb("Bbart", [128, DG, 16], F32, s5)
                CN = sb("CN", [128, DG, 16], F32, s5)
                CtN = sb("CtN", [128, DG, 16], F32, s5)
                s5a = ExitStack()
                are2 = sb("are2", [128, DG], F32, s5a)
                aim2 = sb("aim2", [128, DG], F32, s5a)
                ldt = sb("ldt", [128, DG], F32, s5a)
                lr = sb("lr", [128, DG], F32, s5a)
                li = sb("li", [128, DG], F32, s5a)
                t0 = sb("t0", [128, DG], F32, s5a)
                t1 = sb("t1", [128, DG], F32, s5a)
                t2 = sb("t2", [128, DG], F32, s5a)
                abr = sb("abr", [128, DG], F32, s5a)
                abi = sb("abi", [128, DG], F32, s5a)
                FA = sb("FA", [128, DG], F32, s5a)
                FB = sb("FB", [128, DG], F32, s5a)
                Bst = sb("Bst", [128, DG, 16], F32, s5a)
                Btl = sb("Btl", [128, DG, 16], F32, s5a)
                tb1 = sb("tb1", [128, DG, 16], F32, s5a)
                tb2 = sb("tb2", [128, DG, 16], F32, s5a)
                crow = sb("crow", [128, 128], F32, s5a)
                acat = sb("acat", [64, 128], F32, s5a)
                for (t, src) in ((are2, "s5_a_re"), (aim2, "s5_a_im")):
                    for hf in range(2):
                        P.dma(acat[:, 64 * hf:64 * hf + 64], I[src][l].rearrange("d g p -> (d g) p"), [], [(src, hf)])
                    ps, pk = work.next()
                    P.pe(lambda e, ps=ps: e.transpose(out=ps[:, 0:64], in_=acat[:], identity=ident[0:64, 0:64]), [(src, 0), (src, 1), "ident"], [pk])
                    P.dve(lambda e, ps=ps, t=t: e.tensor_copy(out=t[:], in_=ps[:, 0:64]), [pk], [(src, 0)])
                akeys = [("s5_a_re", 0), ("s5_a_re", 1), ("s5_a_im", 0), ("s5_a_im", 1)]
                P.dma(ldt[:], I["s5_log_dt"][l].rearrange("d g -> (d g)").partition_broadcast(128), [], ["ldt"])
                P.act(lambda e: e.activation(out=ldt[:], in_=ldt[:], func=AF.Exp), ["ldt"], ["ldt"])
                P.dve(lambda e: e.tensor_tensor(out=lr[:], in0=ldt[:], in1=are2[:], op=ALU.mult), ["ldt"] + akeys, ["lr"])
                P.dve(lambda e: e.tensor_tensor(out=li[:], in0=ldt[:], in1=aim2[:], op=ALU.mult), ["ldt"] + akeys, ["li"])
                for k in range(9):
                    P.dve(lambda e, k=k: e.tensor_scalar(out=t0[:], in0=li[:], scalar1=k / TWO_PI, scalar2=None, op0=ALU.mult), ["li"], ["t0"])
                    P.dve(lambda e, k=k: e.tensor_scalar(out=t1[:], in0=li[:], scalar1=k / TWO_PI, scalar2=0.25, op0=ALU.mult, op1=ALU.add), ["li"], ["t1"])
                    sin_turns(PB[:, k, :], t0[:], t2[:], ["t0"], ("PB", k), "t2")
                    sin_turns(PA[:, k, :], t1[:], t2[:], ["t1"], ("PA", k), "t2")
                    P.act(lambda e, k=k: e.activation(out=t0[:], in_=lr[:], func=AF.Exp, scale=float(k)), ["lr", ("PB", k)], ["t0"])
                    P.dve(lambda e, k=k: e.tensor_tensor(out=PA[:, k, :], in0=PA[:, k, :], in1=t0[:], op=ALU.mult), ["t0", ("PA", k)], [("PA", k)])
                    P.dve(lambda e, k=k: e.scalar_tensor_tensor(out=PB[:, k, :], in0=PB[:, k, :], scalar=sgnB[:, 0:1], in1=t0[:], op0=ALU.mult, op1=ALU.mult),
                          ["t0", ("PB", k), "sgnB"], [("PB", k)])
                    if k == 1:
                        P.dve(lambda e: e.tensor_copy(out=abr[:], in_=PA[:, 1, :]), [("PA", 1)], ["abr"])
                        P.dve(lambda e: e.tensor_scalar(out=abi[:], in0=PB[:, 1, :], scalar1=sgnB[:, 0:1], scalar2=None, op0=ALU.mult), [("PB", 1), "sgnB"], ["abi"])
                    if k == 8:
                        P.dve(lambda e: e.tensor_copy(out=rho8[:], in_=t0[:]), ["t0"], ["rho8"])
                PAk = [("PA", k) for k in range(9)]
                PBk = [("PB", k) for k in range(9)]
                P.dve(lambda e: e.tensor_tensor(out=t0[:], in0=are2[:], in1=are2[:], op=ALU.mult), akeys + ["rho8"], ["t0"])
                P.dve(lambda e: e.tensor_tensor(out=t1[:], in0=aim2[:], in1=aim2[:], op=ALU.mult), akeys, ["t1"])
                P.dve(lambda e: e.tensor_tensor(out=t0[:], in0=t0[:], in1=t1[:], op=ALU.add), ["t0", "t1"], ["t0"])
                P.dve(lambda e: e.reciprocal(out=t0[:], in_=t0[:]), ["t0"], ["t0"])
                P.dve(lambda e: e.tensor_scalar(out=abr[:], in0=abr[:], scalar1=-1.0, scalar2=None, op0=ALU.add), ["abr"], ["abr"])
                P.dve(lambda e: e.tensor_tensor(out=t1[:], in0=abr[:], in1=are2[:], op=ALU.mult), ["abr"] + akeys, ["t1"])
                P.dve(lambda e: e.tensor_tensor(out=t2[:], in0=abi[:], in1=aim2[:], op=ALU.mult), ["abi"] + akeys, ["t2"])
                P.dve(lambda e: e.tensor_tensor(out=t1[:], in0=t1[:], in1=t2[:], op=ALU.add), ["t1", "t2"], ["t1"])
                P.dve(lambda e: e.tensor_tensor(out=FA[:], in0=t1[:], in1=t0[:], op=ALU.mult), ["t1", "t0"], ["FA"])
                P.dve(lambda e: e.tensor_tensor(out=t1[:], in0=abi[:], in1=are2[:], op=ALU.mult), ["abi", "FA"] + akeys, ["t1"])
                P.dve(lambda e: e.tensor_tensor(out=t2[:], in0=abr[:], in1=aim2[:], op=ALU.mult), ["abr"] + akeys, ["t2"])
                P.dve(lambda e: e.tensor_tensor(out=t1[:], in0=t1[:], in1=t2[:], op=ALU.subtract), ["t1", "t2"], ["t1"])
                P.dve(lambda e: e.scalar_tensor_tensor(out=FB[:], in0=t1[:], scalar=sgnB[:, 0:1], in1=t0[:], op0=ALU.mult, op1=ALU.mult),
                      ["t1", "t0", "sgnB"], ["FB"])
                P.dve(lambda e: e.tensor_scalar(out=t1[:], in0=li[:], scalar1=8.0 / TWO_PI, scalar2=None, op0=ALU.mult), ["li", "FB"], ["t1"])
                P.dve(lambda e: e.tensor_scalar(out=t2[:], in0=t1[:], scalar1=MAGIC, scalar2=MAGIC, op0=ALU.add, op1=ALU.subtract), ["t1"], ["t2"])
                P.dve(lambda e: e.tensor_tensor(out=tfr[:], in0=t1[:], in1=t2[:], op=ALU.subtract), ["t1", "t2"], ["tfr"])
                bsrc = {0: "s5_b_re", 1: "s5_b_im"}
                for hf in range(2):
                    P.dma(Bst[64 * hf:64 * hf + 64], I[bsrc[hf]][l].rearrange("d g p h -> p (d g) h"), [], [("Bst", hf)])
                    P.dma(Btl[64 * hf:64 * hf + 64], I[bsrc[1 - hf]][l].rearrange("d g p h -> p (d g) h"), [], [("Btl", hf)])
                bk = [("Bst", 0), ("Bst", 1), ("Btl", 0), ("Btl", 1)]
                FAb = FA[:].unsqueeze(2).to_broadcast([128, DG, 16])
                FBb = FB[:].unsqueeze(2).to_broadcast([128, DG, 16])
                P.dve(lambda e: e.tensor_tensor(out=tb1[:], in0=Bst[:], in1=FAb, op=ALU.mult), bk + ["FA"], ["tb1"])
                P.pool(lambda e: e.tensor_tensor(out=tb2[:], in0=Btl[:], in1=FBb, op=ALU.mult), bk + ["FB"], ["tb2"])
                P.dve(lambda e: e.tensor_tensor(out=Bbar[:], in0=tb1[:], in1=tb2[:], op=ALU.add), ["tb1", "tb2"], ["Bbar"])
                P.dve(lambda e: e.tensor_tensor(out=tb1[:], in0=Btl[:], in1=FAb, op=ALU.mult), bk + ["FA", "Bbar"], ["tb1"])
                P.pool(lambda e: e.tensor_tensor(out=tb2[:], in0=Bst[:], in1=FBb, op=ALU.mult), bk + ["FB", "Bbar"], ["tb2"])
                P.dve(lambda e: e.tensor_tensor(out=Bbart[:], in0=tb1[:], in1=tb2[:], op=ALU.subtract), ["tb1", "tb2"], ["Bbart"])
                cre = I["s5_c_re"][l].rearrange("d g h p -> (d g h) p")
                cim = I["s5_c_im"][l].rearrange("d g h p -> (d g h) p")
                for q8 in range(8):
                    P.dma(crow[:, 0:64], cre[128 * q8:128 * q8 + 128, :], [], ["crow0"])
                    P.dma(crow[:, 64:128], cim[128 * q8:128 * q8 + 128, :], [], ["crow1"])
                    ps, pk = work.next()
                    P.pe(lambda e, ps=ps: e.transpose(out=ps[:, 0:128], in_=crow[:], identity=ident[:]), ["crow0", "crow1", "ident"], [pk])
                    dst = CN[:, 8 * q8:8 * q8 + 8, :].rearrange("p g h -> p (g h)")
                    dstt = CtN[:, 8 * q8:8 * q8 + 8, :].rearrange("p g h -> p (g h)")
                    P.dve(lambda e, ps=ps, dst=dst: e.tensor_scalar(out=dst, in0=ps[:, 0:128], scalar1=sgn2[:, 0:1], scalar2=None, op0=ALU.mult), [pk, "sgn2"], ["CN"])
                    P.dve(lambda e, ps=ps, dstt=dstt: e.tensor_copy(out=dstt[0:64, :], in_=ps[64:128, 0:128]), [pk], ["CtN"])
                    P.dve(lambda e, ps=ps, dstt=dstt: e.tensor_scalar(out=dstt[64:128, :], in0=ps[0:64, 0:128], scalar1=-1.0, scalar2=None, op0=ALU.mult), [pk], ["CtN"])
                s5a.close()
                P.barrier()
                if stop == 3:
                    return finish_early()
                qrow = sb("qrow", [128, 2, NCH], F32, s5)
                P.dma(qrow[:], C["qrow"].rearrange("d p c -> p d c"), [], ["qrow"])
                zcol = sb("zcol", [128, 1], BF16, s5)
                P.dve(lambda e: e.memset(zcol[:], 0.0), [], ["zcol"])
                sgn2pi = sb("sgn2pi", [128, 1], F32, s5)
                P.dve(lambda e: e.tensor_scalar(out=sgn2pi[:], in0=sgn2[:], scalar1=TWO_PI, scalar2=None, op0=ALU.mult), ["sgn2"], ["sgn2pi"])
                Gc = sb("Gc", [128, 8], F32, s5)
                BLK = 512
                xblocks = [(c, min(c + BLK, NCH)) for c in range(32, NCH, BLK)]
                CN4 = CN[:].rearrange("p (d g) h -> p d g h", d=2)
                CtN4 = CtN[:].rearrange("p (d g) h -> p d g h", d=2)
                Hind4 = Hind.rearrange("(d g) p c -> d g p c", d=2)
                for gt in range(4):
                    with ExitStack() as sg:
                        T1g = sb("T1g", [128, 2, 8, 8, 16], F32, sg)
                        RSg = sb("RSg", [128, 9, 2, 8, 16], F32, sg)
                        tg1 = sb("tg1", [128, 2, 8, 16], F32, sg)
                        tg2 = sb("tg2", [128, 2, 8, 16], F32, sg)
                        KB = sb("KB", [128, 2, 8, 128], BF16, sg)
                        tK = sb("tK", [128, 128], F32, sg)
                        gsl = slice(8 * gt, 8 * gt + 8)
                        for k in range(9):
                            pa_b = PA[:, k, :].rearrange("p (d g) -> p d g", d=2)[:, :, gsl].unsqueeze(3).to_broadcast([128, 2, 8, 16])
                            pb_b = PB[:, k, :].rearrange("p (d g) -> p d g", d=2)[:, :, gsl].unsqueeze(3).to_broadcast([128, 2, 8, 16])
                            P.dve(lambda e, pa_b=pa_b: e.tensor_tensor(out=tg1[:], in0=CN4[:, :, gsl, :], in1=pa_b, op=ALU.mult), ["CN", ("PA", k)], ["tg1"])
                            P.pool(lambda e, pb_b=pb_b: e.tensor_tensor(out=tg2[:], in0=CtN4[:, :, gsl, :], in1=pb_b, op=ALU.mult), ["CtN", ("PB", k)], ["tg2"])
                            P.dve(lambda e, k=k: e.tensor_tensor(out=RSg[:, k], in0=tg1[:], in1=tg2[:], op=ALU.add), ["tg1", "tg2"], ["RSg"])
                        for d_ in range(2):
                            gs = slice(32 * d_ + 8 * gt, 32 * d_ + 8 * gt + 8)
                            for s in range(8):
                                kin = 7 - s if d_ == 0 else s
                                pa_b = PA[:, kin, gs].unsqueeze(2).to_broadcast([128, 8, 16])
                                pb_b = PB[:, kin, gs].unsqueeze(2).to_broadcast([128, 8, 16])
                                P.dve(lambda e, pa_b=pa_b, gs=gs: e.tensor_tensor(out=tg1[:, 0], in0=Bbar[:, gs, :], in1=pa_b, op=ALU.mult), ["Bbar", ("PA", kin)], ["tg1"])
                                P.pool(lambda e, pb_b=pb_b, gs=gs: e.tensor_tensor(out=tg2[:, 0], in0=Bbart[:, gs, :], in1=pb_b, op=ALU.mult), ["Bbart", ("PB", kin)], ["tg2"])
                                P.dve(lambda e, d_=d_, s=s: e.tensor_tensor(out=T1g[:, d_, s], in0=tg1[:, 0], in1=tg2[:, 0], op=ALU.add), ["tg1", "tg2"], ["T1g"])
                        for d_ in range(2):
                            gs = slice(32 * d_ + 8 * gt, 32 * d_ + 8 * gt + 8)
                            for tau in range(8):
                                ps, pk = work.next()
                                P.pe(lambda e, ps=ps, gs=gs, d_=d_, tau=tau: e.matmul(ps[:, 0:128], lhsT=Bbar[:, gs, :].rearrange("p g h -> p (g h)"),
                                                                                     rhs=RSg[:, tau, d_].rearrange("p g h -> p (g h)"), start=True, stop=True),
                                     ["Bbar", "RSg"], [pk])
                                if d_ == 0 and tau == 0:
                                    P.dve(lambda e, ps=ps: e.tensor_tensor(out=tK[:], in0=ps[:, 0:128], in1=bd16[:], op=ALU.mult), [pk, "bd16"], ["tK"])
                                    P.dve(lambda e, gt=gt: e.scalar_tensor_tensor(out=KB[:, 0, 0, :], in0=ident[:], scalar=s5d[:, gt:gt + 1], in1=tK[:],
                                                                                 op0=ALU.mult, op1=ALU.add), ["tK", "ident", "s5d"], ["KB"])
                                else:
                                    P.dve(lambda e, ps=ps, d_=d_, tau=tau: e.tensor_tensor(out=KB[:, d_, tau, :], in0=ps[:, 0:128], in1=bd16[:], op=ALU.mult),
                                          [pk, "bd16"], ["KB"])
                        with ExitStack() as sw:
                            Wt = sb("Wt", [128, 8, 8, 128], BF16, sw)
                            uTb = [sb("uTb%d" % i, [128, 8 * BLK], BF16, sw) for i in range(2)]
                            uTr = Rot([(uTb[i], "uTb%d" % i) for i in range(2)])
                            tq = sb("tq", [128, BLK], F32, sw); tk = sb("tk", [128, BLK], F32, sw)
                            STs = sb("STs", [128, BLK], F32, sw); CT = sb("CT", [128, BLK], F32, sw)
                            Xs = sb("Xs", [128, BLK], F32, sw); Xt = sb("Xt", [128, BLK], F32, sw)
                            m1 = sb("m1", [128, BLK], F32, sw); m2 = sb("m2", [128, BLK], F32, sw)
                            Gs = sb("Gs", [128, BLK], F32, sw); Gt = sb("Gt", [128, BLK], F32, sw)
                            Hb = [sb("Hb%d" % i, [128, BLK], BF16, sw) for i in range(2)]
                            Hr = Rot([(Hb[i], "Hb%d" % i) for i in range(2)])
                            for d_ in range(2):
                                for half in range(2):
                                    ps, pk = work.next()
                                    for si in range(4):
                                        s = 4 * half + si
                                        P.pe(lambda e, ps=ps, si=si, s=s, d_=d_: e.transpose(out=ps[:, 128 * si:128 * si + 128],
                                                                                            in_=T1g[:, d_, s].rearrange("p g h -> p (g h)"), identity=ident[:]),
                                             ["T1g", "ident"], [pk])
                                    for g8 in range(8):
                                        src = ps[:, :].rearrange("p (s m) -> p s m", m=128)
                                        dst = Wt[:, 4 * half:4 * half + 4, g8, :]
                                        if True:
                                            P.dve(lambda e, src=src, dst=dst, g8=g8: e.tensor_scalar(out=dst, in0=src, scalar1=gmask[:, g8:g8 + 1], scalar2=None, op0=ALU.mult),
                                                  [pk, "gmask"], ["Wt"])
                                        else:
                                            P.act(lambda e, src=src, dst=dst, g8=g8: e.activation(out=dst, in_=src, func=AF.Copy, scale=gmask[:, g8:g8 + 1]),
                                                  [pk, "gmask"], ["Wt"])
                                if d_ == 0:
                                    order = [(0, 32)] + xblocks
                                else:
                                    order = [(0, 32)] + xblocks[::-1]
                                for bi, (c_lo, c_hi) in enumerate(order):
                                    n = c_hi - c_lo
                                    ub, ubk = uTr.next()
                                    P.dma(ub[:, 0:8 * n], uTd[128 * gt:128 * gt + 128, 8 * c_lo:8 * c_hi], ["uT_st_d"], [ubk])
                                    for g8 in range(8):
                                        dg = 32 * d_ + 8 * gt + g8
                                        ps, pk = work.next()
                                        for s in range(8):
                                            P.pe(lambda e, ps=ps, s=s, g8=g8, ub=ub, n=n: e.matmul(ps[:, 0:n], lhsT=Wt[:, s, g8, :], rhs=ub[:, s:8 * n:8],
                                                                                                  start=(s == 0), stop=(s == 7)), ["Wt", ubk], [pk])
                                        tcol = tfr[:, dg:dg + 1]
                                        P.dve(lambda e, d_=d_, c_lo=c_lo, c_hi=c_hi, n=n, tcol=tcol: e.tensor_scalar(out=tq[:, 0:n], in0=qrow[:, d_, c_lo:c_hi], scalar1=tcol,
                                                                                                                   scalar2=None, op0=ALU.mult), ["qrow", "tfr"], ["tq"])
                                        P.dve(lambda e, n=n: e.tensor_scalar(out=tk[:, 0:n], in0=tq[:, 0:n], scalar1=MAGIC, scalar2=MAGIC, op0=ALU.add, op1=ALU.subtract), ["tq"], ["tk"])
                                        P.dve(lambda e, n=n: e.tensor_tensor(out=tq[:, 0:n], in0=tq[:, 0:n], in1=tk[:, 0:n], op=ALU.subtract), ["tq", "tk"], ["tq"])
                                        P.act(lambda e, n=n: e.activation(out=STs[:, 0:n], in_=tq[:, 0:n], func=AF.Sin, scale=sgn2pi[:, 0:1]), ["tq", "sgn2pi"], ["STs"])
                                        P.act(lambda e, n=n: e.activation(out=tk[:, 0:n], in_=tq[:, 0:n], func=AF.Abs), ["tq"], ["tk"])
                                        P.act(lambda e, n=n: e.activation(out=CT[:, 0:n], in_=tk[:, 0:n], func=AF.Sin, scale=-TWO_PI, bias=math.pi / 2), ["tk"], ["CT"])
                                        P.act(lambda e, ps=ps, n=n: e.activation(out=Xs[:, 0:n], in_=ps[:, 0:n], func=AF.Copy), [pk], ["Xs"])
                                        P.act(lambda e, ps=ps, n=n: e.activation(out=Xt[0:64, 0:n], in_=ps[64:128, 0:n], func=AF.Copy), [pk], ["Xt"])
                                        P.act(lambda e, ps=ps, n=n: e.activation(out=Xt[64:128, 0:n], in_=ps[0:64, 0:n], func=AF.Copy), [pk], ["Xt"])
                                        P.dve(lambda e, n=n: e.tensor_tensor(out=m1[:, 0:n], in0=CT[:, 0:n], in1=Xs[:, 0:n], op=ALU.mult), ["CT", "Xs"], ["m1"])
                                        P.pool(lambda e, n=n: e.tensor_tensor(out=m2[:, 0:n], in0=STs[:, 0:n], in1=Xt[:, 0:n], op=ALU.mult), ["STs", "Xt"], ["m2"])
                                        P.dve(lambda e, n=n: e.tensor_tensor(out=m1[:, 0:n], in0=m1[:, 0:n], in1=m2[:, 0:n], op=ALU.add), ["m1", "m2"], ["m1"])
                                        init = 0.0 if bi == 0 else Gc[:, g8:g8 + 1]
                                        rcol = rho8[:, dg:dg + 1].to_broadcast([128, n])
                                        if d_ == 0:
                                            P.dve(lambda e, n=n, init=init, rcol=rcol: e.tensor_tensor_scan(out=Gs[:, 0:n], data0=rcol, data1=m1[:, 0:n], initial=init,
                                                                                                          op0=ALU.mult, op1=ALU.add), ["m1", "rho8", "Gc"], ["Gs"])
                                            lastc = n - 1
                                        else:
                                            P.dve(lambda e, n=n, init=init, rcol=rcol: e.tensor_tensor_scan(out=Gs[:, n - 1::-1], data0=rcol, data1=m1[:, n - 1::-1],
                                                                                                          initial=init, op0=ALU.mult, op1=ALU.add), ["m1", "rho8", "Gc"], ["Gs"])
                                            lastc = 0
                                        P.pool(lambda e, g8=g8, lastc=lastc: e.tensor_copy(out=Gc[:, g8:g8 + 1], in_=Gs[:, lastc:lastc + 1]), ["Gs"], ["Gc"])
                                        P.act(lambda e, n=n: e.activation(out=Gt[0:64, 0:n], in_=Gs[64:128, 0:n], func=AF.Copy), ["Gs"], ["Gt"])
                                        P.act(lambda e, n=n: e.activation(out=Gt[64:128, 0:n], in_=Gs[0:64, 0:n], func=AF.Copy), ["Gs"], ["Gt"])
                                        P.dve(lambda e, n=n: e.tensor_tensor(out=m1[:, 0:n], in0=CT[:, 0:n], in1=Gs[:, 0:n], op=ALU.mult), ["CT", "Gs"], ["m1"])
                                        P.pool(lambda e, n=n: e.tensor_tensor(out=m2[:, 0:n], in0=STs[:, 0:n], in1=Gt[:, 0:n], op=ALU.mult), ["STs", "Gt"], ["m2"])
                                        hb, hbk = Hr.next()
                                        P.dve(lambda e, n=n, hb=hb: e.tensor_tensor(out=hb[:, 0:n], in0=m1[:, 0:n], in1=m2[:, 0:n], op=ALU.subtract), ["m1", "m2"], [hbk])
                                        if d_ == 0:
                                            ncp = min(c_hi, NCH - 1) - c_lo
                                            if ncp > 0:
                                                P.dma(Hind[dg, :, c_lo + 1:c_lo + 1 + ncp], hb[:, 0:ncp], [hbk], ["Hind"])
                                            if c_lo == 0:
                                                P.dma(Hind[dg, :, 0:1], zcol[:], ["zcol"], ["Hind"])
                                        else:
                                            if c_lo == 0:
                                                P.dma(Hind[dg, :, 0:31], hb[:, 1:32], [hbk], ["Hind"])
                                                P.dma(Hind[dg, :, NCH - 1:NCH], hb[:, 0:1], [hbk], ["Hind"])
                                                P.dma(Hind[dg, :, 31:32], zcol[:], ["zcol"], ["Hind"])
                                            elif c_lo == 32:
                                                if n > 1:
                                                    P.dma(Hind[dg, :, 32:32 + n - 1], hb[:, 1:n], [hbk], ["Hind"])
                                            else:
                                                P.dma(Hind[dg, :, c_lo - 1:c_hi - 1], hb[:, 0:n], [hbk], ["Hind"])
                        P.barrier()
                        with ExitStack() as so:
                            Rm = sb("Rm", [128, 2, 8, 8, 128], BF16, so)
                            uTo = [sb("uTo%d" % i, [128, 8 * BLK], BF16, so) for i in range(2)]
                            uTor = Rot([(uTo[i], "uTo%d" % i) for i in range(2)])
                            Hin = [sb("Hin%d" % i, [128, 16, BLK], BF16, so) for i in range(2)]
                            Hinr = Rot([(Hin[i], "Hin%d" % i) for i in range(2)])
                            yst = [sb("yst%d" % i, [128, 8 * BLK], BF16, so) for i in range(2)]
                            ystr = Rot([(yst[i], "yst%d" % i) for i in range(2)])
                            P.pool(lambda e: e.memset(Rm[:], 0.0), [], ["Rm"])
                            for d_ in range(2):
                                for g8 in range(8):
                                    src = RSg[:, 1:9, d_, g8, :] if d_ == 0 else RSg[:, 8:0:-1, d_, g8, :]
                                    P.pool(lambda e, src=src, d_=d_, g8=g8: e.tensor_copy(out=Rm[:, d_, :, g8, 16 * g8:16 * g8 + 16], in_=src), ["RSg"], ["Rm"])
                            for (c_lo, c_hi) in [(0, 32)] + xblocks:
                                n = c_hi - c_lo
                                ub, ubk = uTor.next()
                                P.dma(ub[:, 0:8 * n], uTd[128 * gt:128 * gt + 128, 8 * c_lo:8 * c_hi], ["uT_st_d"], [ubk])
                                hi, hik = Hinr.next()
                                for d_ in range(2):
                                    P.dma(hi[:, 8 * d_:8 * d_ + 8, 0:n], Hind4[d_, 8 * gt:8 * gt + 8, :, c_lo:c_hi].rearrange("g p c -> p g c"), ["Hind"], [(hik, d_)])
                                ys, ysk = ystr.next()
                                for s in range(8):
                                    ps, pk = work.next()
                                    mms = []
                                    for d_ in range(2):
                                        for g8 in range(8):
                                            mms.append((Rm[:, d_, s, g8, :], hi[:, 8 * d_ + g8, 0:n]))
                                    for sp in range(0, s + 1):
                                        mms.append((KB[:, 0, s - sp, :], ub[:, sp:8 * n:8]))
                                    for sp in range(s, 8):
                                        mms.append((KB[:, 1, sp - s, :], ub[:, sp:8 * n:8]))
                                    for i, (lt, rh) in enumerate(mms):
                                        P.pe(lambda e, ps=ps, lt=lt, rh=rh, i=i, n=n, last=len(mms) - 1: e.matmul(ps[:, 0:n], lhsT=lt, rhs=rh, start=(i == 0), stop=(i == last)),
                                             ["Rm", "KB", ubk, (hik, 0), (hik, 1)], [pk])
                                    P.act(lambda e, ps=ps, ys=ys, s=s, n=n: e.activation(out=ys[:, s:8 * n:8], in_=ps[:, 0:n], func=AF.Copy), [pk], [ysk])
                                P.dma(ys5Td[128 * gt:128 * gt + 128, 8 * c_lo:8 * c_hi], ys[:, 0:8 * n], [ysk], ["ys5Td"], q="sp")
                        P.barrier()
            P.barrier()
            if stop == 4:
                return finish_early()

            with ExitStack() as s2:
                masterd = dscr("masterd%d" % l, [8, 128, 1408], BF16)
                wout = sb("wout", [128, 8, D], BF16, s2)
                wglu = sb("wglu", [128, 4, MIX], BF16, s2)
                wbrb = [sb("wbr%d" % i, [128, 4, D], BF16, s2) for i in range(1)]
                wbrr = Rot([(wbrb[i], "wbr%d" % i) for i in range(1)])
                wgb = [sb("wg%d" % i, [128, 8, 512], BF16, s2) for i in range(1)]
                wgr = Rot([(wgb[i], "wg%d" % i) for i in range(1)])
                nakc = sb("nakc", [128, 4, CTXN], BF16, s2); navc = sb("navc", [128, 2, MIX], BF16, s2)
                gkc = sb("gkc", [64, 2, CTXN], BF16, s2); gvc = sb("gvc", [128, 2, 192], BF16, s2)
                winm = sb("winm", [128, 8, 512], BF16, s2)
                P.dma(wout[:], w_out_bf[l].rearrange("(k p) n -> p k n", p=128), [("w_out_bf", l)], ["wout"])
                P.dma(wglu[:], w_glu_bf[l].rearrange("(k p) n -> p k n", p=128), [("w_glu_bf", l)], ["wglu"])
                P.dma(nakc[:], nakTd.rearrange("(c p) n -> p c n", p=128)[:, :, 0:CTXN], ["nakTd"], ["nakc"])
                P.dma(navc[:], navd[0:CTXN, :].rearrange("(s p) c -> p s c", p=128), ["navd"], ["navc"])
                P.dma(gkc[:], gkd[:, :, 0:CTXN], ["gkd"], ["gkc"])
                P.dma(gvc[:], gvd[0:CTXN, :].rearrange("(s p) c -> p s c", p=128), ["gvd"], ["gvc"])
                with ExitStack() as sm:
                    from concourse.ap import AP as _AP
                    Tq = sb("Tq", [64, 23, 64], F32, sm)
                    Tqb = sb("Tqb", [64, 23, 64], BF16, sm)
                    colok = sb("colok", [64, 64], F32, sm)
                    negm = sb("negm", [64, 64], F32, sm)
                    mst = sb("mst", [128, 1408], BF16, sm)
                    P.dma(colok[:], C["colok"][:, :], [], ["colok"])
                    P.dve(lambda e: e.tensor_scalar(out=negm[:], in0=colok[:], scalar1=-1.0, scalar2=-NEGB, op0=ALU.add, op1=ALU.mult), ["colok"], ["negm"])
                    rbpad = dscr("rbpad%d" % l, [120, 127], F32)
                    rbt = sb("rbt", [120, 127], F32, sm)
                    P.dve(lambda e: e.memset(rbt[:], 0.0), [], ["rbt"])
                    P.dma(rbt[:, 48:79], I["na_rel_bias"][l].rearrange("h d j -> (h d) j"), ["rbt"], ["rbt"])
                    P.dma(rbpad[:, :], rbt[:], ["rbt"], ["rbpad"])
                    for h in range(8):
                        P.dve(lambda e: e.memset(Tq[:], NEGB), [], ["Tq"])
                        src = _AP(rbpad.tensor, h * 15 * 127, [[1, 64], [127, 15], [1, 64]])
                        P.dma(Tq[:, 4:19, :], src, ["Tq", "rbpad"], ["Tq"])
                        P.dve(lambda e: e.scalar_tensor_tensor(out=Tq[:], in0=Tq[:], scalar=8.0, in1=colok[:].unsqueeze(1).to_broadcast([64, 23, 64]),
                                                               op0=ALU.mult, op1=ALU.mult), ["Tq", "colok"], ["Tq"])
                        P.dve(lambda e: e.tensor_tensor(out=Tqb[:], in0=Tq[:], in1=negm[:].unsqueeze(1).to_broadcast([64, 23, 64]), op=ALU.add), ["Tq", "negm"], ["Tqb"])
                        for half in range(3):
                            pt, ptk = tb.next()
                            mis = list(range(8 * half, min(22, 8 * half + 8)))
                            for mi in mis:
                                P.pe(lambda e, pt=pt, mi=mi, half=half: e.transpose(out=pt[:, 64 * (mi - 8 * half):64 * (mi - 8 * half) + 64],
                                                                                   in_=Tqb[:, 21 - mi:23 - mi, :].rearrange("p a k -> p (a k)"), identity=identb[0:64, 0:64]),
                                     ["Tqb", "identb"], [ptk])
                            nn = 64 * len(mis)
                            P.dve(lambda e, pt=pt, half=half, nn=nn: e.tensor_copy(out=mst[:, 512 * half:512 * half + nn].rearrange("p (m q) -> p m q", q=64),
                                                                                     in_=pt[:, 0:nn].rearrange("p (m q) -> p m q", q=64)[:, :, ::-1]), [ptk], ["mst"])
                        P.dma(masterd[h], mst[:], ["mst"], ["masterd"])
                P.barrier()

                ntmax = 512
                hT2 = sb("hT2", [128, 8, ntmax], BF16, s2)
                gsil = sb("gsil", [128, 4, ntmax], BF16, s2)
                sigk = sb("sigk", [128, 8, ntmax], BF16, s2)
                ybr = sb("ybr", [128, 4, ntmax], BF16, s2)
                gated = sb("gated", [128, 4, ntmax], BF16, s2)
                merged = sb("merged", [128, 8, ntmax], F32, s2)
                mergb = sb("mergb", [128, 8, ntmax], BF16, s2)
                ftmp = [sb("ftmp%d" % i, [128, ntmax], F32, s2) for i in range(3)]
                ftr = Rot([(ftmp[i], "ftmp%d" % i) for i in range(3)])
                pbuf = [sb("pbuf%d" % i, [128, 512], BF16, s2) for i in range(3)]
                pbr = Rot([(pbuf[i], "pbuf%d" % i) for i in range(3)])
                zzt = sb("zzt", [128, 4, ntmax + 2], BF16, s2); cbt = sb("cbt", [128, 4, ntmax], BF16, s2)
                ys5 = sb("ys5", [128, 4, ntmax], BF16, s2)
                naq = sb("naq", [128, 4, ntmax], BF16, s2); nak = sb("nak", [128, 4, 1024], BF16, s2); nav = sb("nav", [128, 8, MIX], BF16, s2)
                mstb = [sb("mstb%d" % i, [128, 1408], BF16, s2) for i in range(1)]
                mstr = Rot([(mstb[i], "mstb%d" % i) for i in range(1)])
                gq = sb("gq", [64, 8, ntmax], BF16, s2); gk = sb("gk", [64, 2, 768], BF16, s2); gv = sb("gv", [128, 6, 192], BF16, s2)
                xs2 = [sb("xs2_%d" % i, [128, D], F32, s2) for i in range(2)]
                xsr = Rot([(xs2[i], "xs2_%d" % i) for i in range(2)])
                osb = [sb("osb%d" % i, [128, D], F32, s2) for i in range(2)]
                osr = Rot([(osb[i], "osb%d" % i) for i in range(2)])
                rdn = sb("rdn", [128, 512], F32, s2)
                winm_type = [None]
                wsrc = w_in_bf[l].rearrange("(k p) n -> p k n", p=128)

                def proj_gate(col0, nch, func, dest, ntok):
                    for g in range(nch // 4):
                        wg, wgk = wgr.next()
                        P.dma(wg[:], wsrc[:, :, col0 + 512 * g:col0 + 512 * g + 512], [("w_in_bf", l)], [wgk])
                        for c4 in range(4):
                            ps, pk = work.next()
                            for k in range(8):
                                P.pe(lambda e, ps=ps, wg=wg, k=k, c4=c4: e.matmul(ps[:, 0:ntok], lhsT=wg[:, k, 128 * c4:128 * c4 + 128], rhs=hT2[:, k, 0:ntok],
                                                                                 start=(k == 0), stop=(k == 7)), [wgk, "hT2"], [pk])
                            i = 4 * g + c4
                            P.act(lambda e, ps=ps, i=i: e.activation(out=dest[0][:, i, 0:ntok], in_=ps[:, 0:ntok], func=func), [pk], [dest[1]])

                def finish_branch(bi, gate_name, merge_name, ntok):
                    proj_gate(OFF[gate_name], 4, AF.Silu, (gsil, "gsil"), ntok)
                    P.dve(lambda e: e.tensor_tensor(out=gated[:, :, 0:ntok], in0=ybr[:, :, 0:ntok], in1=gsil[:, :, 0:ntok], op=ALU.mult), ["ybr", "gsil"], ["gated"])
                    proj_gate(OFF[merge_name], 8, AF.Sigmoid, (sigk, "sigk"), ntok)
                    wb, wbk = wbrr.next()
                    P.dma(wb[:], w_br_bf[l, bi].rearrange("(k p) n -> p k n", p=128), [("w_br_bf", l)], [wbk])
                    for dc in range(8):
                        ps, pk = work.next()
                        for kc in range(4):
                            P.pe(lambda e, ps=ps, wb=wb, kc=kc, dc=dc: e.matmul(ps[:, 0:ntok], lhsT=wb[:, kc, 128 * dc:128 * dc + 128], rhs=gated[:, kc, 0:ntok],
                                                                               start=(kc == 0), stop=(kc == 3)), [wbk, "gated"], [pk])
                        if bi == 0:
                            P.dve(lambda e, ps=ps, dc=dc: e.tensor_tensor(out=merged[:, dc, 0:ntok], in0=ps[:, 0:ntok], in1=sigk[:, dc, 0:ntok], op=ALU.mult),
                                  [pk, "sigk"], [("merged", dc)])
                        else:
                            ft, ftk = ftr.next()
                            P.dve(lambda e, ps=ps, dc=dc, ft=ft: e.tensor_tensor(out=ft[:, 0:ntok], in0=ps[:, 0:ntok], in1=sigk[:, dc, 0:ntok], op=ALU.mult),
                                  [pk, "sigk"], [ftk])
                            P.pool(lambda e, dc=dc, ft=ft: e.tensor_tensor(out=merged[:, dc, 0:ntok], in0=merged[:, dc, 0:ntok], in1=ft[:, 0:ntok], op=ALU.add),
                                   [ftk, ("merged", dc)], [("merged", dc)])

                def attn_finish(po, pok, pd, pdk, pb, pcols, dcols, nq, dst, extra=None):
                    if extra is None:
                        P.dve(lambda e: e.reciprocal(out=rdn[pb:pb + 64, 0:nq], in_=pd[pb:pb + 64, dcols]), [pdk], ["rdn"])
                    else:
                        P.dve(lambda e: e.tensor_scalar(out=rdn[pb:pb + 64, 0:nq], in0=pd[pb:pb + 64, dcols], scalar1=extra, scalar2=None, op0=ALU.add), [pdk, "esink"], ["rdn"])
                        P.dve(lambda e: e.reciprocal(out=rdn[pb:pb + 64, 0:nq], in_=rdn[pb:pb + 64, 0:nq]), ["rdn"], ["rdn"])
                    P.dve(lambda e: e.tensor_tensor(out=dst, in0=po[pb:pb + 64, pcols], in1=rdn[pb:pb + 64, 0:nq], op=ALU.mult), [pok, "rdn"], ["ybr"])

                p2tiles = ([("ctx", 0, CTXN)] if l < DEPTH - 1 else []) + [t for t in tiles if t[0] == "x"]
                for (kind, n0, ntok) in p2tiles:
                    nsub = ntok // 128
                    mi = 1 if kind == "ctx" else 0
                    t0_ = n0 - CTXN
                    P.dma(hT2[:, :, 0:ntok], hTd.rearrange("(k p) n -> p k n", p=128)[:, :, n0:n0 + ntok], ["hTd"], ["hT2"])
                    P.dma(ys5[:, :, 0:ntok], ys5Td.rearrange("(c p) n -> p c n", p=128)[:, :, n0:n0 + ntok], ["ys5Td"], ["ys5"])
                    for cc in range(4):
                        f0, f0k = ftr.next()
                        f1, f1k = ftr.next()
                        P.pool(lambda e, cc=cc, f0=f0: e.tensor_tensor(out=f0[:, 0:ntok], in0=ys5[:, cc, 0:ntok], in1=ys5[:, cc, 0:ntok], op=ALU.mult), ["ys5"], [f0k])
                        P.dve(lambda e, f0=f0: e.tensor_scalar(out=f0[:, 0:ntok], in0=f0[:, 0:ntok], scalar1=0.044715, scalar2=1.0, op0=ALU.mult, op1=ALU.add), [f0k], [f0k])
                        P.dve(lambda e, cc=cc, f0=f0: e.tensor_tensor(out=f0[:, 0:ntok], in0=f0[:, 0:ntok], in1=ys5[:, cc, 0:ntok], op=ALU.mult), [f0k, "ys5"], [f0k])
                        P.act(lambda e, f0=f0, f1=f1: e.activation(out=f1[:, 0:ntok], in_=f0[:, 0:ntok], func=AF.Sigmoid, scale=1.5957691216057308), [f0k], [f1k])
                        P.dve(lambda e, cc=cc, f1=f1: e.tensor_tensor(out=gated[:, cc, 0:ntok], in0=f1[:, 0:ntok], in1=ys5[:, cc, 0:ntok], op=ALU.mult), [f1k, "ys5"], ["gated"])
                    for oc in range(4):
                        ps, pk = work.next()
                        for kc in range(4):
                            P.pe(lambda e, ps=ps, kc=kc, oc=oc: e.matmul(ps[:, 0:ntok], lhsT=wglu[:, kc, 128 * oc:128 * oc + 128], rhs=gated[:, kc, 0:ntok],
                                                                        start=(kc == 0), stop=(kc == 3)), ["wglu", "gated"], [pk])
                        f1, f1k = ftr.next()
                        P.act(lambda e, ps=ps, f1=f1: e.activation(out=f1[:, 0:ntok], in_=ps[:, 0:ntok], func=AF.Sigmoid), [pk], [f1k])
                        P.dve(lambda e, oc=oc, f1=f1: e.tensor_tensor(out=ybr[:, oc, 0:ntok], in0=f1[:, 0:ntok], in1=gated[:, oc, 0:ntok], op=ALU.mult), [f1k, "gated"], ["ybr"])
                    finish_branch(0, "s5_gate", "merge_s5", ntok)
                    seq_lo = 0 if kind == "ctx" else CTXN
                    seq_hi = CTXN if kind == "ctx" else NT
                    lo = max(n0 - 1, seq_lo); hi = min(n0 + ntok + 1, seq_hi)
                    if lo == n0:
                        P.dve(lambda e: e.memset(zzt[:, :, 0:1], 0.0), [], ["zzt"])
                    if hi == n0 + ntok:
                        P.dve(lambda e: e.memset(zzt[:, :, ntok + 1:ntok + 2], 0.0), [], ["zzt"])
                    P.dma(zzt[:, :, lo - (n0 - 1):hi - (n0 - 1)], zzTd.rearrange("(c p) n -> p c n", p=128)[:, :, lo:hi], ["zz_st_d", "zzt"], ["zzt"])
                    P.dma(cbt[:, :, 0:ntok], cbTd.rearrange("(c p) n -> p c n", p=128)[:, :, n0:n0 + ntok], ["cb_st_d"], ["cbt"])
                    for cc in range(4):
                        f0, f0k = ftr.next()
                        P.dve(lambda e, cc=cc, f0=f0: e.tensor_scalar(out=f0[:, 0:ntok], in0=zzt[:, cc, 0:ntok], scalar1=convw[:, cc, 0:1], scalar2=convb[:, cc:cc + 1],
                                                                    op0=ALU.mult, op1=ALU.add), ["zzt", ("convw", 0), ("convw", 1), ("convw", 2), "convb"], [f0k])
                        P.dve(lambda e, cc=cc, f0=f0: e.scalar_tensor_tensor(out=f0[:, 0:ntok], in0=zzt[:, cc, 1:ntok + 1], scalar=convw[:, cc, 1:2], in1=f0[:, 0:ntok],
                                                                           op0=ALU.mult, op1=ALU.add), ["zzt", ("convw", 0), ("convw", 1), ("convw", 2), f0k], [f0k])
                        P.dve(lambda e, cc=cc, f0=f0: e.scalar_tensor_tensor(out=f0[:, 0:ntok], in0=zzt[:, cc, 2:ntok + 2], scalar=convw[:, cc, 2:3], in1=f0[:, 0:ntok],
                                                                           op0=ALU.mult, op1=ALU.add), ["zzt", ("convw", 0), ("convw", 1), ("convw", 2), f0k], [f0k])
                        P.dve(lambda e, cc=cc, f0=f0: e.tensor_tensor(out=ybr[:, cc, 0:ntok], in0=f0[:, 0:ntok], in1=cbt[:, cc, 0:ntok], op=ALU.mult), [f0k, "cbt"], ["ybr"])
                    finish_branch(1, "conv_gate", "merge_conv", ntok)
                    P.dma(naq[:, :, 0:ntok], naqTd.rearrange("(c p) n -> p c n", p=128)[:, :, n0:n0 + ntok], ["naqTd"], ["naq"])
                    if kind == "x":
                        r0 = t0_ // 64
                        jv = [j for j in range(8) if 0 <= r0 - 4 + 2 * j <= ROWS - 2]
                        jl, jh = jv[0], jv[-1] + 1
                        tk0 = CTXN + (r0 - 4 + 2 * jl) * 64
                        nk = (jh - jl) * 128
                        P.dma(nak[:, :, 128 * jl:128 * jh], nakTd.rearrange("(c p) n -> p c n", p=128)[:, :, tk0:tk0 + nk], ["nakTd"], ["nak"])
                        P.dma(nav[:, jl:jh, :], navd[tk0:tk0 + nk, :].rearrange("(j p) c -> p j c", p=128), ["navd"], ["nav"])
                        ti = t0_ // 512
                        wt = 1 if ti == 0 else (2 if ti == NXT - 1 else 0)
                        if winm_type[0] != wt:
                            winm_type[0] = wt
                            for j in range(8):
                                ft, ftk = ftr.next()
                                P.dma(ft[:, 0:512], C["winmask"][wt, j], [], [ftk])
                                P.dve(lambda e, j=j, ft=ft: e.tensor_copy(out=winm[:, j, :], in_=ft[:, 0:512]), [ftk], ["winm"])
                        lat = [("lat", j) for j in jv]
                    else:
                        lat = []
                    blocks = lat + [("ctx", 0), ("ctx", 1)]
                    for h in range(8):
                        hc, pb = h // 2, 64 * (h % 2)
                        if kind == "x":
                            ms, msk = mstr.next()
                            P.dma(ms[:], masterd[h], ["masterd"], [msk])
                        po, pok = accA
                        pd, pdk = accB
                        for bi_, (bk, j) in enumerate(blocks):
                            ps, pk = work.next()
                            if bk == "lat":
                                P.pe(lambda e, ps=ps, j=j, hc=hc, pb=pb: e.matmul(ps[:, 0:ntok], lhsT=nak[pb:pb + 64, hc, 128 * j:128 * j + 128], rhs=naq[pb:pb + 64, hc, 0:ntok],
                                                                                 start=True, stop=False), ["nak", "naq"], [pk])
                                P.pe(lambda e, ps=ps, j=j, ms=ms: e.matmul(ps[:, 0:ntok], lhsT=identb[:], rhs=ms[:, 64 * (14 - 2 * j):64 * (22 - 2 * j)], start=False, stop=True),
                                     ["identb", msk], [pk])
                                vl = nav[:, j, 128 * hc:128 * hc + 128]
                                vkey = "nav"
                            else:
                                P.pe(lambda e, ps=ps, j=j, hc=hc, pb=pb: e.matmul(ps[:, 0:ntok], lhsT=nakc[pb:pb + 64, hc, 128 * j:128 * j + 128], rhs=naq[pb:pb + 64, hc, 0:ntok],
                                                                                 start=True, stop=True), ["nakc", "naq"], [pk])
                                vl = navc[:, j, 128 * hc:128 * hc + 128]
                                vkey = "navc"
                            pbf, pbk = pbr.next()
                            P.act(lambda e, ps=ps, pbf=pbf: e.activation(out=pbf[:, 0:ntok], in_=ps[:, 0:ntok], func=AF.Exp, scale=0.125), [pk], [pbk])
                            if bk == "lat":
                                P.pool(lambda e, pbf=pbf, j=j: e.tensor_tensor(out=pbf[:, 0:ntok], in0=pbf[:, 0:ntok], in1=winm[:, j, 0:ntok], op=ALU.mult), [pbk, "winm"], [pbk])
                            st_, sp_ = (bi_ == 0), (bi_ == len(blocks) - 1)
                            P.pe(lambda e, vl=vl, pbf=pbf, st_=st_, sp_=sp_: e.matmul(po[:, 0:ntok], lhsT=vl, rhs=pbf[:, 0:ntok], start=st_, stop=sp_), [vkey, pbk], [pok])
                            P.pe(lambda e, pbf=pbf, st_=st_, sp_=sp_: e.matmul(pd[:, 0:ntok], lhsT=onesb[:], rhs=pbf[:, 0:ntok], start=st_, stop=sp_), ["onesb", pbk], [pdk])
                        attn_finish(po, pok, pd, pdk, pb, slice(0, ntok), slice(0, ntok), ntok, ybr[pb:pb + 64, hc, 0:ntok])
                    finish_branch(2, "na_gate", "merge_na", ntok)
                    P.dma(gq[:, :, 0:ntok], gqd[:, :, n0:n0 + ntok], ["gqd"], ["gq"])
                    if kind == "x":
                        klo = max(t0_ - 128, 0); khi = min(t0_ + ntok + 128, SEQ)
                        off = klo - (t0_ - 128)
                        P.dma(gk[:, :, off:off + (khi - klo)], gkd[:, :, CTXN + klo:CTXN + khi], ["gkd"], ["gk"])
                        P.dma(gv[:, off // 128:(off + khi - klo) // 128, :], gvd[CTXN + klo:CTXN + khi, :].rearrange("(j p) c -> p j c", p=128), ["gvd"], ["gv"])
                    for kv in range(2):
                        for qb in range(nsub):
                            if kind == "x":
                                gblk = t0_ // 128 + qb
                                lat = [dk for dk in (-1, 0, 1) if 0 <= gblk + dk < SEQ // 128]
                            else:
                                lat = []
                            blocks = [("lat", dk) for dk in lat] + [("ctx", 0), ("ctx", 1)]
                            po, pok = accA
                            pd, pdk = accB
                            qr = gq[:, 4 * kv:4 * kv + 4, 128 * qb:128 * qb + 128]
                            for bi_, (bk, j) in enumerate(blocks):
                                ps, pk = work.next()
                                if bk == "lat":
                                    b6 = qb + 1 + j
                                    kl = gk[:, kv, 128 * b6:128 * b6 + 128]
                                    va, vb_ = gv[:, b6, 0:128], gv[:, b6, 64:192]
                                    kkey, vkey = "gk", "gv"
                                else:
                                    kl = gkc[:, kv, 128 * j:128 * j + 128]
                                    va, vb_ = gvc[:, j, 0:128], gvc[:, j, 64:192]
                                    kkey, vkey = "gkc", "gvc"
                                P.pe(lambda e, ps=ps, kl=kl, qr=qr: e.matmul(ps[:, :], lhsT=kl, rhs=qr, start=True, stop=True), [kkey, "gq"], [pk])
                                pbf, pbk = pbr.next()
                                P.act(lambda e, ps=ps, pbf=pbf: e.activation(out=pbf[:, :], in_=ps[:, :], func=AF.Exp, scale=0.125), [pk], [pbk])
                                if bk == "lat" and j != 0:
                                    tsel = 0 if j == -1 else 1
                                    P.pool(lambda e, pbf=pbf, tsel=tsel: e.tensor_tensor(out=pbf[:, :].rearrange("p (h q) -> p h q", q=128), in0=pbf[:, :].rearrange("p (h q) -> p h q", q=128),
                                                                                       in1=trib[:, tsel, :].unsqueeze(1).to_broadcast([128, 4, 128]), op=ALU.mult), [pbk, "trib"], [pbk])
                                st_, sp_ = (bi_ == 0), (bi_ == len(blocks) - 1)
                                p4 = pbf[:, :].rearrange("p (h q) -> p h q", q=128)
                                v_even, v_odd = (va, vb_) if kv == 0 else (vb_, va)
                                P.pe(lambda e, v_even=v_even, p4=p4, st_=st_, sp_=sp_: e.matmul(po[:, 0:256], lhsT=v_even, rhs=p4[:, 0:4:2, :], start=st_, stop=sp_), [vkey, pbk], [pok])
                                P.pe(lambda e, v_odd=v_odd, p4=p4, st_=st_, sp_=sp_: e.matmul(accC[0][:, 0:256], lhsT=v_odd, rhs=p4[:, 1:4:2, :], start=st_, stop=sp_), [vkey, pbk], [accC[1]])
                                P.pe(lambda e, pbf=pbf, st_=st_, sp_=sp_: e.matmul(pd[:, :], lhsT=onesb[:], rhs=pbf[:, :], start=st_, stop=sp_), ["onesb", pbk], [pdk])
                            for i in range(4):
                                hq = 4 * kv + i
                                pb = 64 * (i % 2)
                                pc = (i // 2) * 128
                                po_i, pok_i = (po, pok) if i % 2 == 0 else accC
                                attn_finish(po_i, pok_i, pd, pdk, pb, slice(pc, pc + 128), slice(128 * i, 128 * i + 128), 128,
                                            ybr[pb:pb + 64, hq // 2, 128 * qb:128 * qb + 128], extra=esink[pb:pb + 64, hq:hq + 1])
                    finish_branch(3, "gqa_gate", "merge_gqa", ntok)
                    for dc in range(8):
                        P.act(lambda e, dc=dc: e.activation(out=mergb[:, dc, 0:ntok], in_=merged[:, dc, 0:ntok], func=AF.Copy), [("merged", dc)], ["mergb"])
                    src, skey = resid_src(l, kind)
                    dst, dkey = resid_dst(l, kind)
                    rr0 = n0 if kind == "ctx" else t0_
                    for s in range(nsub):
                        xs, xsk = xsr.next()
                        P.dma(xs[:], src[rr0 + 128 * s:rr0 + 128 * s + 128, :], [skey], [xsk])
                        ob, obk = osr.next()
                        for half in range(2):
                            ps, pk = work.next()
                            for k in range(8):
                                P.pe(lambda e, ps=ps, k=k, s=s, half=half: e.matmul(ps[:, :], lhsT=mergb[:, k, 128 * s:128 * s + 128], rhs=wout[:, k, 512 * half:512 * half + 512],
                                                                                   start=(k == 0), stop=(k == 7)), ["mergb", "wout"], [pk])
                            P.dve(lambda e, ps=ps, ob=ob, half=half, mi=mi: e.tensor_tensor(out=ob[:, 512 * half:512 * half + 512], in0=ps[:, :], in1=gateR[:, mi, 512 * half:512 * half + 512],
                                                                                          op=ALU.mult), [pk, "gateR"], [obk])
                        P.pool(lambda e, ob=ob, xs=xs: e.tensor_tensor(out=ob[:], in0=ob[:], in1=xs[:], op=ALU.add), [obk, xsk], [obk])
                        P.dma(dst[rr0 + 128 * s:rr0 + 128 * s + 128, :], ob[:], [obk], [dkey], q="sp")
            P.barrier()

        stats = P.emit(final_keys=["y"])
    return nc, stats


class _Stop(Exception):
    pass


_CACHE = {}


def run(inputs, SEQ, DEPTH, debug=(), n_cores=8, stop=99):
    key = (SEQ, DEPTH, tuple(debug), stop)
    if key not in _CACHE:
        _CACHE[key] = build(SEQ, DEPTH, debug, stop)
    nc, stats = _CACHE[key]
    consts = host_consts(SEQ)
    in_maps = []
    for i in range(n_cores):
        b = i % 2
        m = {}
        for k, v in inputs.items():
            v = np.ascontiguousarray(np.asarray(v, dtype=np.float32))
            if k in ("x", "c", "ctx"):
                m[k] = np.ascontiguousarray(v[b])
            else:
                m[k] = v
        for k, v in consts.items():
            m["k_" + k] = v
        in_maps.append(m)
    res = run_bass_kernel_spmd(nc, in_maps, core_ids=list(range(n_cores)))
    return res, stats


def kernel(**inputs):
    res, _ = run(inputs, 16384, 4)
    out = np.stack([np.asarray(res.results[0]["y"]), np.asarray(res.results[1]["y"])]).astype(np.float32)
    return out
```
